# Optimizing a Trainium2 kernel written in Bass

```python
import math
import jax, jax.numpy as jnp
from jax import lax
import numpy as np

D_MODEL = 1024
BATCH = 8
SEQ = 2048
DEPTH = 1
DEC_BATCH = 128
DEC_SEQ = 4
PAST_LEN = 16384
PAGE_SIZE = 128

GM_WIDTH = D_MODEL
GM_GROUPS = 8
GM_GROUP_DIM = GM_WIDTH // GM_GROUPS
GM_CHUNK = 128
DN_HEADS = 8
DN_HEAD_DIM = D_MODEL // DN_HEADS
DN_WIDTH = DN_HEADS * DN_HEAD_DIM
DN_CONV = 4
DN_CHUNK = 64
D_FF = 2816
IN_COLS = 2 * GM_WIDTH + 4 * DN_WIDTH + 2 * DN_HEADS + 2 * D_MODEL
ALPHA = (2.0 * DEPTH) ** 0.25
BETA_INIT = (8.0 * DEPTH) ** -0.25
LN_EPS = 1e-5
RMS_EPS = 1e-6

kernel_name = 'hybrid_gmlp_gdn_macaron_deepnorm_step'


def layer_norm(x, g, b):
    x32 = x.astype(jnp.float32)
    mu = jnp.mean(x32, axis=-1, keepdims=True)
    var = jnp.mean(jnp.square(x32 - mu), axis=-1, keepdims=True)
    return ((x32 - mu) * lax.rsqrt(var + LN_EPS) * g + b).astype(x.dtype)


def rms_norm(x, w):
    x32 = x.astype(jnp.float32)
    return x32 * lax.rsqrt(jnp.mean(jnp.square(x32), axis=-1, keepdims=True) + RMS_EPS) * w


def l2_normalize(x):
    return x * lax.rsqrt(jnp.sum(jnp.square(x), axis=-1, keepdims=True) + RMS_EPS)


def swiglu_ffn(x, w_up, w_down):
    a, gt = jnp.split(x @ w_up, 2, axis=-1)
    return (jax.nn.silu(a) * gt) @ w_down


def chunk_spatial_mix(v, w_s, b_s):
    B, T, _ = v.shape
    L = min(GM_CHUNK, T)
    vc = v.reshape(B, T // L, L, GM_GROUPS, GM_GROUP_DIM)
    w = jnp.tril(w_s[:, :L, :L])
    mixed = jnp.einsum('gts,bcsgd->bctgd', w, vc) + b_s[:, :L].T[None, None, :, :, None]
    return mixed.reshape(B, T, GM_WIDTH)


def _to_chunks(a, n, L, pad):
    a = jnp.pad(a, [(0, 0), (0, pad)] + [(0, 0)] * (a.ndim - 2))
    a = a.reshape((a.shape[0], n, L) + a.shape[2:])
    return jnp.swapaxes(jnp.swapaxes(a, 0, 1), 2, 3)


def gated_delta_rule(q, k, v, g_log, beta, s0):
    B, T, H, _ = q.shape
    dv = v.shape[-1]
    L = min(DN_CHUNK, T)
    n = -(-T // L)
    pad = n * L - T
    q, k, v = _to_chunks(q, n, L, pad), _to_chunks(k, n, L, pad), _to_chunks(v, n, L, pad)
    g_log, beta = _to_chunks(g_log, n, L, pad), _to_chunks(beta, n, L, pad)
    G = jnp.cumsum(g_log, axis=-1)
    causal = jnp.tril(jnp.ones((L, L), dtype=bool))
    strict = jnp.tril(jnp.ones((L, L), dtype=bool), -1)
    diff = G[..., :, None] - G[..., None, :]
    decay = jnp.where(causal, jnp.exp(jnp.where(causal, diff, 0.0)), 0.0)
    kb = k * beta[..., None]
    a_mat = jnp.where(strict, jnp.einsum('nbhid,nbhjd->nbhij', kb, k) * decay, 0.0)
    eye = jnp.eye(L, dtype=q.dtype)
    t_inv = lax.linalg.triangular_solve(eye + a_mat, jnp.broadcast_to(eye, a_mat.shape),
                                        left_side=True, lower=True, unit_diagonal=True)
    u_vals = jnp.einsum('nbhij,nbhjv->nbhiv', t_inv, v * beta[..., None])
    w_vals = jnp.einsum('nbhij,nbhjd->nbhid', t_inv, kb * jnp.exp(G)[..., None])
    qk = jnp.where(causal, jnp.einsum('nbhid,nbhjd->nbhij', q, k) * decay, 0.0)

    def step(S, xs):
        q_c, k_c, u_c, w_c, g_c, qk_c = xs
        v_new = u_c - jnp.einsum('bhld,bhdv->bhlv', w_c, S)
        o = (jnp.einsum('bhld,bhdv->bhlv', q_c * jnp.exp(g_c)[..., None], S)
             + jnp.einsum('bhij,bhjv->bhiv', qk_c, v_new))
        g_last = g_c[..., -1:]
        S = (S * jnp.exp(g_last)[..., None]
             + jnp.einsum('bhld,bhlv->bhdv', k_c * jnp.exp(g_last - g_c)[..., None], v_new))
        return S, o

    s_final, o = lax.scan(step, s0, (q, k, u_vals, w_vals, G, qk))
    o = jnp.swapaxes(jnp.swapaxes(o, 2, 3), 0, 1).reshape(B, n * L, H, dv)[:, :T]
    return o, s_final


def token_mixing(h, conv_state, ssm_state, p):
    B, T, _ = h.shape
    proj = h @ p['w_in'] + p['b_in']
    offs = np.cumsum([GM_WIDTH, GM_WIDTH, 3 * DN_WIDTH, DN_WIDTH, DN_HEADS, DN_HEADS]).tolist()
    u, v, qkv, z, beta_logit, decay_logit, gates = jnp.split(proj, offs, axis=-1)

    u = jax.nn.gelu(u)
    v = layer_norm(jax.nn.gelu(v), p['gm_v_g'], p['gm_v_b'])
    y_a = u * chunk_spatial_mix(v, p['gm_w_s'], p['gm_b_s']).astype(u.dtype)

    xc = jnp.concatenate([conv_state.astype(qkv.dtype), qkv], axis=1)
    conv_new = xc[:, T:]
    w_c = p['dn_conv_w']
    acc = xc[:, 0:T] * w_c[0]
    for i in range(1, DN_CONV):
        acc = acc + xc[:, i:i + T] * w_c[i]
    qkv_c = jax.nn.silu(acc).astype(jnp.float32)
    q, k, vd = jnp.split(qkv_c, 3, axis=-1)
    q = l2_normalize(q.reshape(B, T, DN_HEADS, DN_HEAD_DIM)) * (DN_HEAD_DIM ** -0.5)
    k = l2_normalize(k.reshape(B, T, DN_HEADS, DN_HEAD_DIM))
    vd = vd.reshape(B, T, DN_HEADS, DN_HEAD_DIM)
    beta = jax.nn.sigmoid(beta_logit.astype(jnp.float32))
    g_log = (-jnp.exp(p['dn_a_log'].astype(jnp.float32))
             * jax.nn.softplus(decay_logit.astype(jnp.float32) + p['dn_dt_bias'].astype(jnp.float32)))
    o, ssm_new = gated_delta_rule(q, k, vd, g_log, beta, ssm_state.astype(jnp.float32))
    o = rms_norm(o, p['dn_norm_w']) * jax.nn.silu(z.astype(jnp.float32).reshape(B, T, DN_HEADS, DN_HEAD_DIM))
    y_b = o.reshape(B, T, DN_WIDTH).astype(h.dtype)

    gate_a, gate_b = jnp.split(gates, 2, axis=-1)
    merged = (jax.nn.sigmoid(gate_a) * (y_a @ p['w_branch_a'])
              + jax.nn.sigmoid(gate_b) * (y_b @ p['w_branch_b']))
    return merged @ p['w_out'], v, conv_new, ssm_new.astype(ssm_state.dtype)


def trunk_layer(x, conv_state, ssm_state, p):
    x = layer_norm(ALPHA * x + 0.5 * swiglu_ffn(x, p['ffn1_w_up'], p['ffn1_w_down']), p['ln1_g'], p['ln1_b'])
    mix, v_rows, conv_new, ssm_new = token_mixing(x, conv_state, ssm_state, p)
    x = layer_norm(ALPHA * x + mix, p['ln2_g'], p['ln2_b'])
    x = layer_norm(ALPHA * x + 0.5 * swiglu_ffn(x, p['ffn2_w_up'], p['ffn2_w_down']), p['ln3_g'], p['ln3_b'])
    return x, v_rows, conv_new, ssm_new


def setup_inputs(seed: int = 0) -> dict:
    key = jax.random.key(seed)
    ks = jax.random.split(key, 32)
    f32 = jnp.float32

    def nrm(k, shape, scale):
        return jax.random.normal(k, shape, f32) * scale

    def gain(k, n):
        return 1.0 + 0.02 * jax.random.normal(k, (DEPTH, n), f32)

    def bias(k, n):
        return 0.02 * jax.random.normal(k, (DEPTH, n), f32)

    dt = jnp.exp(jax.random.uniform(ks[17], (DEPTH, DN_HEADS), f32, math.log(1e-3), math.log(1e-1)))
    return {
        'x_prompt': nrm(ks[0], (BATCH, SEQ, D_MODEL), 1.0),
        'x_sample': nrm(ks[1], (DEC_BATCH, DEC_SEQ, D_MODEL), 1.0),
        'state_conv': nrm(ks[2], (DEPTH, DEC_BATCH, DN_CONV - 1, 3 * DN_WIDTH), 1.0),
        'state_ssm': nrm(ks[3], (DEPTH, DEC_BATCH, DN_HEADS, DN_HEAD_DIM, DN_HEAD_DIM), 0.1),
        'ffn1_w_up': nrm(ks[4], (DEPTH, D_MODEL, 2 * D_FF), D_MODEL ** -0.5),
        'ffn1_w_down': nrm(ks[5], (DEPTH, D_FF, D_MODEL), BETA_INIT * D_FF ** -0.5),
        'ln1_g': gain(ks[6], D_MODEL),
        'ln1_b': bias(ks[7], D_MODEL),
        'w_in': nrm(ks[8], (DEPTH, D_MODEL, IN_COLS), D_MODEL ** -0.5),
        'b_in': bias(ks[9], IN_COLS),
        'gm_v_g': gain(ks[10], GM_WIDTH),
        'gm_v_b': bias(ks[11], GM_WIDTH),
        'gm_w_s': nrm(ks[12], (DEPTH, GM_GROUPS, GM_CHUNK, GM_CHUNK), GM_CHUNK ** -0.5),
        'gm_b_s': 1.0 + 0.02 * jax.random.normal(ks[13], (DEPTH, GM_GROUPS, GM_CHUNK), f32),
        'dn_conv_w': nrm(ks[14], (DEPTH, DN_CONV, 3 * DN_WIDTH), DN_CONV ** -0.5),
        'dn_a_log': jnp.log(jax.random.uniform(ks[15], (DEPTH, DN_HEADS), f32, 1.0, 16.0)),
        'dn_dt_bias': dt + jnp.log(-jnp.expm1(-dt)),
        'dn_norm_w': gain(ks[16], DN_HEAD_DIM),
        'w_branch_a': nrm(ks[18], (DEPTH, GM_WIDTH, D_MODEL), GM_WIDTH ** -0.5),
        'w_branch_b': nrm(ks[19], (DEPTH, DN_WIDTH, D_MODEL), DN_WIDTH ** -0.5),
        'w_out': nrm(ks[20], (DEPTH, D_MODEL, D_MODEL), BETA_INIT * D_MODEL ** -0.5),
        'ln2_g': gain(ks[21], D_MODEL),
        'ln2_b': bias(ks[22], D_MODEL),
        'ffn2_w_up': nrm(ks[23], (DEPTH, D_MODEL, 2 * D_FF), D_MODEL ** -0.5),
        'ffn2_w_down': nrm(ks[24], (DEPTH, D_FF, D_MODEL), BETA_INIT * D_FF ** -0.5),
        'ln3_g': gain(ks[25], D_MODEL),
        'ln3_b': bias(ks[26], D_MODEL),
    }


def reference(x_prompt, x_sample, state_conv, state_ssm, ffn1_w_up, ffn1_w_down, ln1_g, ln1_b,
              w_in, b_in, gm_v_g, gm_v_b, gm_w_s, gm_b_s, dn_conv_w, dn_a_log, dn_dt_bias, dn_norm_w,
              w_branch_a, w_branch_b, w_out, ln2_g, ln2_b, ffn2_w_up, ffn2_w_down, ln3_g, ln3_b):
    y_p, y_s = x_prompt, x_sample
    conv_p, ssm_p, conv_s, ssm_s, v_s = [], [], [], [], []
    for l in range(DEPTH):
        p = {
            'ffn1_w_up': ffn1_w_up[l], 'ffn1_w_down': ffn1_w_down[l], 'ln1_g': ln1_g[l], 'ln1_b': ln1_b[l],
            'w_in': w_in[l], 'b_in': b_in[l], 'gm_v_g': gm_v_g[l], 'gm_v_b': gm_v_b[l],
            'gm_w_s': gm_w_s[l], 'gm_b_s': gm_b_s[l], 'dn_conv_w': dn_conv_w[l], 'dn_a_log': dn_a_log[l],
            'dn_dt_bias': dn_dt_bias[l], 'dn_norm_w': dn_norm_w[l], 'w_branch_a': w_branch_a[l],
            'w_branch_b': w_branch_b[l], 'w_out': w_out[l], 'ln2_g': ln2_g[l], 'ln2_b': ln2_b[l],
            'ffn2_w_up': ffn2_w_up[l], 'ffn2_w_down': ffn2_w_down[l], 'ln3_g': ln3_g[l], 'ln3_b': ln3_b[l],
        }
        zero_conv = jnp.zeros((x_prompt.shape[0], DN_CONV - 1, 3 * DN_WIDTH), x_prompt.dtype)
        zero_ssm = jnp.zeros((x_prompt.shape[0], DN_HEADS, DN_HEAD_DIM, DN_HEAD_DIM), state_ssm.dtype)
        y_p, _, c_p, s_p = trunk_layer(y_p, zero_conv, zero_ssm, p)
        y_s, v_rows, c_s, s_s = trunk_layer(y_s, state_conv[l], state_ssm[l], p)
        conv_p.append(c_p)
        ssm_p.append(s_p)
        conv_s.append(c_s)
        ssm_s.append(s_s)
        v_s.append(v_rows)
    return (y_p, y_s, jnp.stack(conv_p), jnp.stack(ssm_p), jnp.stack(conv_s), jnp.stack(ssm_s), jnp.stack(v_s))
```

```python
import math
from contextlib import ExitStack

import numpy as np
import concourse.bass as bass
import concourse.mybir as mybir
from concourse.bass_utils import run_bass_kernel_spmd

import os
CSTOP = int(os.environ.get('CSTOP', '0'))
OVERLAP = int(os.environ.get('OVERLAP', '1'))
F32 = mybir.dt.float32
BF16 = mybir.dt.bfloat16
AF = mybir.ActivationFunctionType
ALU = mybir.AluOpType

D = 1024
DFF = 2816
NJ = 22
ALPHA = 2.0 ** 0.25
LN_EPS = 1e-5
RMS_EPS = 1e-6
GELU_K = math.sqrt(2.0 / math.pi)

OFF_U, OFF_V, OFF_Q, OFF_Z, OFF_BD, OFF_GA, OFF_GB = 0, 1024, 2048, 5120, 6144, 6160, 7184
SM_BFM, SM_CONV, SM_NRM, SM_BBD, SM_ALOG, SM_DTB, NSM = 0, 56, 152, 153, 169, 177, 192
MK_ID, MK_GMP, MK_GMS, MK_CAUP, MK_STRP, MK_CAUS, MK_STRS, MK_SUFP, MK_SUFS, MK_SEL, MK_ID2, NMK = (
    0, 128, 256, 320, 384, 448, 512, 576, 640, 704, 720, 784)


class Buf:
    __slots__ = ("name", "w", "r", "excl")

    def __init__(self, name="", excl=False):
        self.name = name
        self.w = None
        self.r = {}
        self.excl = excl


class Sched:
    ENGS = ("pe", "act", "dve", "pool", "sp")

    def __init__(self, nc, stack):
        self.nc = nc
        self.ops = {e: [] for e in self.ENGS}
        self.sem = {}
        self.cnt = {}
        self.known = {e: {} for e in self.ENGS}
        self.stack = stack
        for e in self.ENGS:
            self._mksem("E_" + e)
        self.rings = {}

    def _mksem(self, key):
        self.sem[key] = self.stack.enter_context(self.nc.semaphore(key))
        self.cnt[key] = 0

    def _deps(self, reads, writes):
        deps = {}

        def add(ev):
            if ev is None:
                return
            k, v = ev
            if deps.get(k, 0) < v:
                deps[k] = v
        for b in reads:
            add(b.w)
        for b in writes:
            add(b.w)
            for k, v in b.r.items():
                add((k, v))
        return deps

    def _waits(self, e, deps, self_sync):
        out = []
        kn = self.known[e]
        for k, v in deps.items():
            if k == "E_" + e and not self_sync:
                continue
            if kn.get(k, 0) >= v:
                continue
            kn[k] = v
            out.append((k, v))
        return out

    def _mark(self, ev, reads, writes):
        k, v = ev
        for b in reads:
            if b.excl:
                if b not in writes:
                    b.w = ev
                    b.r = {}
                continue
            if b.r.get(k, 0) < v:
                b.r[k] = v
        for b in writes:
            b.w = ev
            b.r = {}

    def op(self, e, fn, reads=(), writes=()):
        deps = self._deps(reads, writes)
        if e == "pool":
            self._merge_bar(deps)
        waits = self._waits(e, deps, self_sync=(e != "pe"))
        k = "E_" + e
        self.cnt[k] += 1
        ev = (k, self.cnt[k])
        self.ops[e].append((waits, fn, k, 1))
        self._mark(ev, reads, writes)
        return ev

    def dma(self, e, fn, reads=(), writes=(), ring="io", nring=8):
        if ring not in self.rings:
            keys = []
            for i in range(nring):
                key = "D_%s_%d" % (ring, i)
                self._mksem(key)
                keys.append(key)
            self.rings[ring] = [keys, 0]
        keys, idx = self.rings[ring]
        key = keys[idx % len(keys)]
        self.rings[ring][1] = idx + 1
        deps = self._deps(reads, writes)
        if e == "pool" and ring != "w":
            self._merge_bar(deps)
        if self.cnt[key] > 0 and deps.get(key, 0) < self.cnt[key]:
            deps[key] = self.cnt[key]
        waits = self._waits(e, deps, self_sync=True)
        self.cnt[key] += 16
        ev = (key, self.cnt[key])
        self.ops[e].append((waits, fn, key, 16))
        self._mark(ev, reads, writes)
        return ev

    def _merge_bar(self, deps):
        for k, v in getattr(self, "bar_deps", {}).items():
            if deps.get(k, 0) < v:
                deps[k] = v

    def barrier(self, final=False):
        deps = {k: v for k, v in self.cnt.items() if v > 0}
        self.bar_deps = dict(deps)
        for e in self.ENGS:
            if e == "pool" and not final:
                continue
            waits = self._waits(e, dict(deps), self_sync=True)
            if waits:
                self.ops[e].append((waits, None, None, 0))

    def emit(self):
        sem = self.sem
        with self.nc.Block() as block:
            def mk(e):
                def body(eng):
                    for waits, fn, k, inc in self.ops[e]:
                        for (wk, wv) in waits:
                            eng.wait_ge(sem[wk], wv)
                        if fn is not None:
                            ins = fn(eng)
                            ins.then_inc(sem[k], inc)
                return body
            block.tensor(mk("pe"))
            block.scalar(mk("act"))
            block.vector(mk("dve"))
            block.gpsimd(mk("pool"))
            block.sync(mk("sp"))


def _pieces(n, step=512):
    out = []
    c = 0
    while c < n:
        out.append((c, min(step, n - c)))
        c += step
    return out


def build(NPT, groups, has_s=True, stop_after=None):
    assert sum(groups) == NPT
    nc = bass.Bass("TRN2", target_bir_lowering=False)

    def din(name, shape):
        return nc.dram_tensor(name, list(shape), F32, kind="ExternalInput").ap()

    def dout(name, shape):
        return nc.dram_tensor(name, list(shape), F32, kind="ExternalOutput").ap()

    xp = din("xp", [NPT * 128, D])
    xs = din("xs", [64, D])
    sconv = din("sconv", [48, 3072])
    sssm = din("sssm", [16, 8, 128, 128])
    w_up = [din("w_up1", [2 * NJ, 128, 8, 128]), din("w_up2", [2 * NJ, 128, 8, 128])]
    w_dn = [din("w_dn1", [DFF, D]), din("w_dn2", [DFF, D])]
    w_in = din("w_in", [56, 128, 8, 128])
    w_v = din("w_v", [128, 8, D])
    w_bd = din("w_bd", [128, 8, 16])
    w_ba = din("w_ba", [8, 128, 8, 128])
    w_bb = din("w_bb", [8, 128, 8, 128])
    w_o = din("w_o", [128, 8, D])
    lnbc = din("lnbc", [9, 128, D])
    small_d = din("small", [128, NSM])
    masks_d = din("masks", [128, NMK])
    wsT_d = din("wsT", [128, 8, 128])
    wsTs_d = din("wsTs", [64, 8, 64])
    bsr_d = din("bsr", [1, 8, 128])
    bsrs_d = din("bsrs", [1, 8, 64])

    yp = dout("yp", [NPT * 128, D])
    ys = dout("ys", [64, D])
    ncp = dout("ncp", [3, 3072])
    nsp = dout("nsp", [8, 128, 128])
    ncs = dout("ncs", [48, 3072])
    nss = dout("nss", [16, 8, 128, 128])
    ngv = dout("ngv", [64, D])

    GTP = max(groups)
    GT = GTP + (1 if has_s else 0)
    TPG = GTP * 128
    TG = TPG + (64 if has_s else 0)
    NR = 8

    with ExitStack() as st:
        S = Sched(nc, st)

        uid = [0]

        def sb(name, shape, dt, stack=st):
            uid[0] += 1
            return stack.enter_context(nc.sbuf_tensor("%s_%d" % (name, uid[0]), list(shape), dt))

        def ps(name, shape, dt):
            return st.enter_context(nc.psum_tensor(name, list(shape), dt))

        def A(fn, r=(), w=()):
            return S.op("act", fn, r, w)

        def V(fn, r=(), w=()):
            return S.op("dve", fn, r, w)

        def P(fn, r=(), w=()):
            return S.op("pe", fn, r, w)

        def G(fn, r=(), w=()):
            return S.op("pool", fn, r, w)

        R = sb("R", [128, GT, D], F32)
        bR = [Buf("R%d" % i) for i in range(GT)]
        XT = sb("XT", [128, 8, TG], BF16)
        bXT = [Buf("XT%d" % i) for i in range(GT)]
        ring = sb("ring", [128, NR, 8, 128], BF16)
        bring = [Buf("ring%d" % i) for i in range(NR)]
        gbb = sb("gbb", [128, 2, D], F32)
        bgbb = [Buf("gbb0"), Buf("gbb1")]
        MT = sb("MT", [128, 8, TG], F32)
        bMT = Buf("MT")
        small = sb("small", [128, NSM], F32)
        bsmall = Buf("small")
        masks = sb("masks", [128, NMK], F32)
        bmasks = Buf("masks")
        identb = sb("identb", [128, 128], BF16)
        onesf = sb("onesf", [128, 128], F32)
        onesb = sb("onesb", [1, 128], BF16)
        nbfm = sb("nbfm", [128, 56], F32)
        nea = sb("nea", [128, 8], F32)
        bconst = Buf("const")
        wsTm = sb("wsTm", [128, 8, 128], BF16)
        wsTsm = sb("wsTsm", [64, 8, 64], BF16)
        bsb = sb("bsb", [1, 8, 128], BF16)
        bssb = sb("bssb", [1, 8, 64], BF16)
        Wbd = sb("Wbd", [128, 8, 16], BF16)
        bWbd = Buf("Wbd")
        CARRY = sb("CARRY", [128, 24, 3], F32)
        bCARRY = Buf("CARRY")
        Sst = sb("Sst", [128, 8, 128], F32)
        Sbf = sb("Sbf", [128, 8, 128], BF16)
        bSst = Buf("Sst")
        bSbf = Buf("Sbf")
        mv = sb("mv", [128, GT, 4], F32)
        bmv = Buf("mv")
        st6 = sb("st6", [128, GT, 2, 6], F32)
        bst6 = Buf("st6")
        xb16 = sb("xb16", [128, D], BF16)
        bxb16 = Buf("xb16")

        PB = [ps("PB%d" % i, [128, 512], F32) for i in range(4)]
        bPB = [Buf("PB%d" % i, True) for i in range(4)]
        PS5 = ps("PS5", [128, 512], F32)
        bPS5 = Buf("PS5", True)
        PS6 = ps("PS6", [128, 512], F32)
        bPS6 = Buf("PS6", True)
        PT = ps("PT", [128, 8, 128], BF16)
        bPT = Buf("PT", True)
        PS7 = ps("PS7", [128, 512], F32)
        bPS7 = Buf("PS7", True)

        state = {"rs": 0, "pb": 0}

        def next_bank():
            i = state["pb"] % 4
            state["pb"] += 1
            return PB[i], bPB[i]

        def stream_block(src):
            i = state["rs"] % NR
            state["rs"] += 1
            S.dma("pool", lambda e, i=i, src=src: e.dma_start(
                out=ring[:, i, :, :], in_=src),
                writes=[bring[i]], ring="w", nring=NR)
            return ring[:, i, :, :], bring[i]

        def id_f(n):
            return masks[0:n, MK_ID:MK_ID + n]

        S.dma("sp", lambda e: e.dma_start(out=small[:], in_=small_d), writes=[bsmall])
        S.dma("sp", lambda e: e.dma_start(out=masks[:], in_=masks_d), writes=[bmasks])
        with ExitStack() as cst:
            wsTf = sb("wsTf", [128, 8, 128], F32, cst)
            wsTsf = sb("wsTsf", [64, 8, 64], F32, cst)
            bsf = sb("bsf", [1, 8, 128], F32, cst)
            bssf = sb("bssf", [1, 8, 64], F32, cst)
            btmp = Buf("ctmp")
            S.dma("sp", lambda e: e.dma_start(out=wsTf[:], in_=wsT_d), writes=[btmp])
            S.dma("sp", lambda e: e.dma_start(out=wsTsf[:], in_=wsTs_d), writes=[btmp])
            S.dma("sp", lambda e: e.dma_start(out=bsf[:], in_=bsr_d), writes=[btmp])
            S.dma("sp", lambda e: e.dma_start(out=bssf[:], in_=bsrs_d), writes=[btmp])
            V(lambda e: e.tensor_copy(out=identb[:], in_=masks[:, MK_ID:MK_ID + 128]), [bmasks], [bconst])
            V(lambda e: e.memset(onesf[:], 1.0), [], [bconst])
            V(lambda e: e.memset(onesb[:], 1.0), [], [bconst])
            V(lambda e: e.tensor_scalar_mul(out=nbfm[:], in0=small[:, SM_BFM:SM_BFM + 56], scalar1=-1.0),
              [bsmall], [bconst])
            A(lambda e: e.activation(out=nea[:], in_=small[:, SM_ALOG:SM_ALOG + 8], func=AF.Exp), [bsmall], [bconst])
            V(lambda e: e.tensor_scalar_mul(out=nea[:], in0=nea[:], scalar1=-1.0), [bconst], [bconst])
            V(lambda e: e.tensor_tensor(
                out=wsTm[:], in0=wsTf[:],
                in1=masks[:, MK_GMP:MK_GMP + 128].unsqueeze(1).to_broadcast([128, 8, 128]), op=ALU.mult),
              [btmp, bmasks], [bconst])
            V(lambda e: e.tensor_tensor(
                out=wsTsm[:], in0=wsTsf[:],
                in1=masks[0:64, MK_GMS:MK_GMS + 64].unsqueeze(1).to_broadcast([64, 8, 64]), op=ALU.mult),
              [btmp, bmasks], [bconst])
            V(lambda e: e.tensor_copy(out=bsb[:], in_=bsf[:]), [btmp], [bconst])
            V(lambda e: e.tensor_copy(out=bssb[:], in_=bssf[:]), [btmp], [bconst])
            V(lambda e: e.memset(CARRY[:], 0.0), [], [bCARRY])
            V(lambda e: e.memset(Sst[:], 0.0), [], [bSst])
            V(lambda e: e.memset(Sbf[:], 0.0), [], [bSbf])
            V(lambda e: e.memset(mv[:], 1.0), [], [bmv])
            S.dma("pool", lambda e: e.dma_start(
                out=Wbd[:], in_=w_bd),
                writes=[bWbd], ring="wr", nring=8)
            S.barrier()

        def to_xt(li, col0, pt):
            A(lambda e: e.activation(out=xb16[0:pt, :], in_=R[0:pt, li, :], func=AF.Copy), [bR[li]], [bxb16])

            def tr(e):
                for k in range(8):
                    ins = e.transpose(out=PT[:, k, 0:pt], in_=xb16[0:pt, k * 128:(k + 1) * 128],
                                      identity=identb[0:pt, 0:pt])
                return ins
            P(tr, [bxb16, bconst], [bPT])
            V(lambda e: e.tensor_copy(out=XT[:, :, col0:col0 + pt], in_=PT[:, :, 0:pt]), [bPT], [bXT[li]])

        def load_gb(gi, bi):
            S.dma("sp", lambda e: e.dma_start(out=gbb[:, 0, :], in_=lnbc[gi]), writes=[bgbb[0]])
            S.dma("sp", lambda e: e.dma_start(out=gbb[:, 1, :], in_=lnbc[bi]), writes=[bgbb[1]])

        def layer_norm(tl, src_of, dst_of, bsrc, bdst):
            for (li, col0, pt) in tl:
                def bn(e, li=li, pt=pt):
                    e.bn_stats(out=st6[0:pt, li, 0, :], in_=src_of(li, pt)[:, 0:512])
                    return e.bn_stats(out=st6[0:pt, li, 1, :], in_=src_of(li, pt)[:, 512:1024])
                V(bn, [bsrc[li]], [bst6])
                V(lambda e, li=li, pt=pt: e.bn_aggr(out=mv[0:pt, li, 0:2], in_=st6[0:pt, li, :, :]), [bst6], [bmv])
            lo = min(t[0] for t in tl)
            hi = max(t[0] for t in tl) + 1
            A(lambda e: e.activation(out=mv[:, lo:hi, 2], in_=mv[:, lo:hi, 1], func=AF.Ln, bias=LN_EPS), [bmv], [bmv])
            A(lambda e: e.activation(out=mv[:, lo:hi, 3], in_=mv[:, lo:hi, 2], func=AF.Exp, scale=-0.5), [bmv], [bmv])
            for (li, col0, pt) in tl:
                V(lambda e, li=li, pt=pt: e.tensor_scalar(
                    out=dst_of(li, pt), in0=src_of(li, pt), scalar1=mv[0:pt, li, 0:1], scalar2=mv[0:pt, li, 3:4],
                    op0=ALU.subtract, op1=ALU.mult), [bmv, bsrc[li]], [bdst[li]])
                V(lambda e, li=li, pt=pt: e.tensor_tensor(
                    out=dst_of(li, pt), in0=dst_of(li, pt), in1=gbb[0:pt, 0, :], op=ALU.mult),
                  [bgbb[0], bdst[li]], [bdst[li]])
                V(lambda e, li=li, pt=pt: e.tensor_tensor(
                    out=dst_of(li, pt), in0=dst_of(li, pt), in1=gbb[0:pt, 1, :], op=ALU.add),
                  [bgbb[1], bdst[li]], [bdst[li]])

        def gemm_fm(blocks, rhsT, rbufs_of_piece, pieces, epi, after_block=None):
            for bi, src in enumerate(blocks):
                if after_block is not None and bi > 0:
                    after_block(bi - 1)
                slot, bslot = stream_block(src)
                for (c0, n) in pieces:
                    bank, bbank = next_bank()

                    def mm(e, slot=slot, bank=bank, c0=c0, n=n):
                        for k in range(8):
                            ins = e.matmul(bank[:, 0:n], lhsT=slot[:, k, :], rhs=rhsT[:, k, c0:c0 + n],
                                           start=(k == 0), stop=(k == 7))
                        return ins
                    P(mm, [bslot] + rbufs_of_piece(c0, n), [bbank])
                    epi(bi, c0, n, bank, bbank)

        ngroups = len(groups)

        def do_group(gidx, gsz, tile_base):
            last = (gidx == ngroups - 1)
            sam = has_s and last
            Tpg = gsz * 128
            Tg = Tpg + (64 if sam else 0)
            tl = [(li, li * 128, 128) for li in range(gsz)]
            if sam:
                tl.append((gsz, Tpg, 64))
            ppieces = _pieces(Tpg)
            pieces = list(ppieces) + ([(Tpg, 64)] if sam else [])

            def xt_bufs(c0, n, tl=tl):
                return [bXT[li] for (li, col0, pt) in tl if col0 < c0 + n and col0 + pt > c0]

            for (li, col0, pt) in tl:
                if pt == 128:
                    r0 = (tile_base + li) * 128
                    S.dma("sp", lambda e, li=li, r0=r0: e.dma_start(out=R[:, li, :], in_=xp[r0:r0 + 128, :]),
                          writes=[bR[li]])
                else:
                    S.dma("sp", lambda e, li=li: e.dma_start(out=R[0:64, li, :], in_=xs), writes=[bR[li]])
                to_xt(li, col0, pt)

            def ffn(which, final):
                wu, wd = w_up[which], w_dn[which]
                load_gb(0 if which == 0 else 4, 1 if which == 0 else 5)
                with ExitStack() as sc:
                    HT = sb("HT", [128, NJ, TG], BF16, sc)
                    bHT = [Buf("HT%d" % j) for j in range(NJ)]
                    Wdn = sb("Wdn", [128, NJ, D], BF16, sc)
                    bWdn = [Buf("Wdn%d" % j) for j in range(NJ)]
                    tA = [sb("tA%d" % i, [128, 512], F32, sc) for i in range(2)]
                    btA = [Buf("tA0"), Buf("tA1")]
                    tB = [sb("tB%d" % i, [128, 512], F32, sc) for i in range(2)]
                    btB = [Buf("tB0"), Buf("tB1")]
                    cnt = 0
                    for j in range(NJ):
                        sa, bsa = stream_block(wu[j])
                        sg, bsg = stream_block(wu[NJ + j])
                        S.dma("pool", lambda e, j=j: e.dma_start(out=Wdn[:, j, :], in_=wd[j * 128:(j + 1) * 128, :]),
                              writes=[bWdn[j]], ring="wr", nring=8)
                        for (c0, n) in pieces:
                            ba, bba = next_bank()
                            bg, bbg = next_bank()

                            def mm(e, slot=sa, bank=ba, c0=c0, n=n):
                                for k in range(8):
                                    ins = e.matmul(bank[:, 0:n], lhsT=slot[:, k, :], rhs=XT[:, k, c0:c0 + n],
                                                   start=(k == 0), stop=(k == 7))
                                return ins
                            P(mm, [bsa] + xt_bufs(c0, n), [bba])

                            def mm2(e, slot=sg, bank=bg, c0=c0, n=n):
                                for k in range(8):
                                    ins = e.matmul(bank[:, 0:n], lhsT=slot[:, k, :], rhs=XT[:, k, c0:c0 + n],
                                                   start=(k == 0), stop=(k == 7))
                                return ins
                            P(mm2, [bsg] + xt_bufs(c0, n), [bbg])
                            ta, bta = tA[cnt % 2], btA[cnt % 2]
                            tb, btb = tB[cnt % 2], btB[cnt % 2]
                            cnt += 1
                            A(lambda e, ta=ta, ba=ba, n=n: e.activation(out=ta[:, 0:n], in_=ba[:, 0:n], func=AF.Silu),
                              [bba], [bta])
                            V(lambda e, ta=ta, bg=bg, j=j, c0=c0, n=n: e.scalar_tensor_tensor(
                                out=HT[:, j, c0:c0 + n], in0=ta[:, 0:n], scalar=0.5, in1=bg[:, 0:n],
                                op0=ALU.mult, op1=ALU.mult), [bta, bbg], [bHT[j]])
                    def down_tile(li, col0, pt):
                        for half in range(2):
                            bank, bbank = next_bank()

                            def mmd(e, bank=bank, col0=col0, pt=pt, half=half):
                                for j in range(NJ):
                                    ins = e.matmul(bank[0:pt, :], lhsT=HT[:, j, col0:col0 + pt],
                                                   rhs=Wdn[:, j, half * 512:(half + 1) * 512],
                                                   start=(j == 0), stop=(j == NJ - 1))
                                return ins
                            P(mmd, bHT + bWdn, [bbank])
                            V(lambda e, bank=bank, li=li, pt=pt, half=half: e.scalar_tensor_tensor(
                                out=R[0:pt, li, half * 512:(half + 1) * 512],
                                in0=R[0:pt, li, half * 512:(half + 1) * 512], scalar=ALPHA, in1=bank[0:pt, :],
                                op0=ALU.mult, op1=ALU.add), [bbank, bR[li]], [bR[li]])

                    def finish_tile(li, col0, pt):
                        layer_norm([(li, col0, pt)], lambda li_, pt_: R[0:pt_, li_, :],
                                   lambda li_, pt_: R[0:pt_, li_, :], bR, bR)
                        if final:
                            if pt == 128:
                                r0 = (tile_base + li) * 128
                                S.dma("sp", lambda e, li=li, r0=r0: e.dma_start(out=yp[r0:r0 + 128, :],
                                                                               in_=R[:, li, :]),
                                      reads=[bR[li]], writes=[Buf()])
                            else:
                                S.dma("sp", lambda e, li=li: e.dma_start(out=ys, in_=R[0:64, li, :]),
                                      reads=[bR[li]], writes=[Buf()])
                        else:
                            to_xt(li, col0, pt)
                    prev = None
                    for t_ in tl:
                        down_tile(*t_)
                        if prev is not None:
                            finish_tile(*prev)
                        prev = t_
                    finish_tile(*prev)
                    S.barrier()

            ffn(0, final=(stop_after == "ffn1"))
            if stop_after == "ffn1":
                return

            with ExitStack() as sB:
                ZT = sb("ZT", [128, 8, TG], BF16, sB)
                bZT = Buf("ZT")
                with ExitStack() as sc:
                    UT = sb("UT", [128, 8, TG], BF16, sc)
                    bUT = Buf("UT")
                    WsmA = sb("Wsm", [128, 8, D], BF16, sc)
                    bWsmA = Buf("Wsm")
                    t0A = sb("t0", [128, 512], F32, sc)
                    t1A = sb("t1", [128, 512], F32, sc)
                    bt0A, bt1A = Buf("t0"), Buf("t1")
                    vx = sb("vx", [128, D], F32, sc)
                    vy = sb("vy", [128, D], F32, sc)
                    bvx, bvy = [Buf("vx")], [Buf("vy")]
                    VnA = sb("VnA", [128, GT, D], BF16, sc)
                    bVnA = [Buf("VnA%d" % i) for i in range(GT)]
                    bvb = sb("bvb", [128, D], F32, sc)
                    bbvb = Buf("bvb")
                    S.dma("pool", lambda e: e.dma_start(
                        out=WsmA[:], in_=w_v),
                        writes=[bWsmA], ring="wr", nring=8)
                    S.dma("sp", lambda e: e.dma_start(out=bvb[:], in_=lnbc[8]), writes=[bbvb])
                    load_gb(6, 7)

                    def gelu_chain(x, sq, n, out_ap, rb, wb, bx_, bsq_):
                        V(lambda e: e.scalar_tensor_tensor(out=sq, in0=sq, scalar=0.044715, in1=x,
                                                           op0=ALU.mult, op1=ALU.mult), [bx_, bsq_], [bsq_])
                        V(lambda e: e.tensor_tensor(out=sq, in0=sq, in1=x, op=ALU.add), [bx_, bsq_], [bsq_])
                        A(lambda e: e.activation(out=sq, in_=sq, func=AF.Exp, scale=-2.0 * GELU_K), [bsq_], [bsq_])
                        V(lambda e: e.tensor_scalar_add(out=sq, in0=sq, scalar1=1.0), [bsq_], [bsq_])
                        V(lambda e: e.reciprocal(out=sq, in_=sq), [bsq_], [bsq_])
                        V(lambda e: e.tensor_tensor(out=out_ap, in0=sq, in1=x, op=ALU.mult), [bx_, bsq_] + rb, wb)

                    def epi_u(bi, c0, n, bank, bbank):
                        A(lambda e: e.activation(out=UT[:, bi, c0:c0 + n], in_=bank[:, 0:n], func=AF.Gelu_apprx_tanh,
                                                 bias=small[:, SM_BFM + bi:SM_BFM + bi + 1]),
                          [bbank, bsmall], [bUT])
                    def v_front(li, col0, pt):
                        for half in range(2):
                            bank, bbank = next_bank()

                            def mmv(e, bank=bank, col0=col0, pt=pt, half=half):
                                for k in range(8):
                                    ins = e.matmul(bank[0:pt, :], lhsT=XT[:, k, col0:col0 + pt],
                                                   rhs=WsmA[:, k, half * 512:(half + 1) * 512],
                                                   start=(k == 0), stop=(k == 7))
                                return ins
                            P(mmv, [bXT[li], bWsmA], [bbank])
                            V(lambda e, bank=bank, pt=pt, half=half: e.tensor_tensor(
                                out=vx[0:pt, half * 512:(half + 1) * 512], in0=bank[0:pt, :],
                                in1=bvb[0:pt, half * 512:(half + 1) * 512], op=ALU.add), [bbank, bbvb], bvx)
                        A(lambda e, pt=pt: e.activation(out=vx[0:pt, :], in_=vx[0:pt, :], func=AF.Gelu_apprx_tanh),
                          bvx, bvx)
                        layer_norm([(0, 0, pt)], lambda li_, pt_: vx[0:pt_, :], lambda li_, pt_: vy[0:pt_, :],
                                   bvx, bvy)
                        if pt == 64:
                            S.dma("sp", lambda e: e.dma_start(out=ngv, in_=vy[0:64, :]), reads=bvy, writes=[Buf()])
                        A(lambda e, pt=pt, li=li: e.activation(out=VnA[0:pt, li, :], in_=vy[0:pt, :], func=AF.Copy),
                          bvy, [bVnA[li]])

                    def v_back(li, col0, pt):
                        wm = wsTm if pt == 128 else wsTsm
                        bs_ = bsb if pt == 128 else bssb
                        for g0 in (0, 4):
                            bank, bbank = next_bank()

                            def mmg(e, bank=bank, pt=pt, g0=g0, wm=wm, bs_=bs_, li=li):
                                for g in range(g0, g0 + 4):
                                    o_ = bank[:, (g - g0) * 128:(g - g0) * 128 + pt]
                                    e.matmul(o_, lhsT=VnA[0:pt, li, g * 128:(g + 1) * 128], rhs=wm[0:pt, g, 0:pt],
                                             start=True, stop=False)
                                    ins = e.matmul(o_, lhsT=onesb[0:1, 0:128], rhs=bs_[0:1, g, 0:pt],
                                                   start=False, stop=True)
                                return ins
                            P(mmg, [bVnA[li], bconst], [bbank])
                            V(lambda e, bank=bank, pt=pt, g0=g0, col0=col0: e.tensor_tensor(
                                out=UT[:, g0:g0 + 4, col0:col0 + pt], in0=UT[:, g0:g0 + 4, col0:col0 + pt],
                                in1=bank[:, :].rearrange("p (g t) -> p g t", g=4)[:, :, 0:pt], op=ALU.mult),
                              [bbank, bUT], [bUT])

                    vq = list(tl)

                    def after_u(bi):
                        if vq:
                            v_front(*vq.pop(0))
                    gemm_fm([w_in[c] for c in range(8)], XT, xt_bufs,
                            pieces, epi_u, after_block=after_u)
                    while vq:
                        v_front(*vq.pop(0))
                    for t_ in tl:
                        v_back(*t_)

                    def epi_ga(bi, c0, n, bank, bbank):
                        A(lambda e: e.activation(out=MT[:, bi, c0:c0 + n], in_=bank[:, 0:n], func=AF.Sigmoid,
                                                 bias=small[:, SM_BFM + 40 + bi:SM_BFM + 41 + bi]),
                          [bbank, bsmall], [bMT])
                    gemm_fm([w_in[40 + c] for c in range(8)], XT, xt_bufs,
                            pieces, epi_ga)

                    def epi_ba(bi, c0, n, bank, bbank):
                        V(lambda e: e.tensor_tensor(out=MT[:, bi, c0:c0 + n], in0=MT[:, bi, c0:c0 + n],
                                                    in1=bank[:, 0:n], op=ALU.mult), [bbank, bMT], [bMT])
                    gemm_fm([w_ba[c] for c in range(8)], UT, lambda c0, n: [bUT],
                            pieces, epi_ba)
                    S.barrier()
                if stop_after == "ba":
                    return

                with ExitStack() as sQ:
                    QT = sb("QT", [128, 8, TG], BF16, sQ)
                    KT = sb("KT", [128, 8, TG], BF16, sQ)
                    VT = sb("VT", [128, 8, TG], BF16, sQ)
                    bQT, bKT, bVT = Buf("QT"), Buf("KT"), Buf("VT")
                    NCS = sb("NCS", [128, 24, 48], F32, sQ)
                    bNCS = Buf("NCS")
                    with ExitStack() as sc:
                        PREs = [sb("PRE", [128, 3 + TPG], F32, sc) for _ in range(4)]
                        bPREs = [Buf("PRE%d" % i) for i in range(4)]
                        PRESs = [sb("PRES", [128, 16, 7], F32, sc) for _ in range(4)]
                        bPRESs = [Buf("PRES%d" % i) for i in range(4)]
                        accs_ = [sb("acc", [128, TG], F32, sc) for _ in range(4)]
                        baccs = [Buf("acc%d" % i) for i in range(4)]
                        exs = [sb("ex", [128, TG], F32, sc) for _ in range(4)]
                        bexs = [Buf("ex%d" % i) for i in range(4)]
                        t0Bs = [sb("t0b", [128, 512], F32, sc) for _ in range(2)]
                        bt0Bs = [Buf("t0b0"), Buf("t0b1")]
                        rrs = [sb("rr", [128, TG], F32, sc) for _ in range(4)]
                        brrs = [Buf("rr%d" % i) for i in range(4)]
                        if sam:
                            SCin = sb("SCin", [48, 3072], F32, sc)
                            bSCin = Buf("SCin")
                            SCT = sb("SCT", [128, 24, 48], F32, sc)
                            bSCT = Buf("SCT")
                            S.dma("sp", lambda e: e.dma_start(out=SCin[:], in_=sconv), writes=[bSCin])
                            for c in range(24):
                                P(lambda e, c=c: e.transpose(out=PS5[:, 0:48], in_=SCin[0:48, c * 128:(c + 1) * 128],
                                                             identity=id_f(48)), [bSCin, bmasks], [bPS5])
                                V(lambda e, c=c: e.tensor_copy(out=SCT[:, c, :], in_=PS5[:, 0:48]), [bPS5], [bSCT])
                        def qkv_block(c):
                            PRE, bPRE = PREs[c % 4], bPREs[c % 4]
                            PRES, bPRES = PRESs[c % 4], bPRESs[c % 4]
                            acc, bacc = accs_[c % 4], baccs[c % 4]
                            ex, bex = exs[c % 4], bexs[c % 4]
                            t0B, bt0B = t0Bs[c % 2], bt0Bs[c % 2]
                            slot, bslot = stream_block(w_in[8 + c])
                            bcol = small[:, SM_BFM + 8 + c:SM_BFM + 9 + c]
                            cw = [small[:, SM_CONV + c * 4 + i:SM_CONV + c * 4 + i + 1] for i in range(4)]
                            V(lambda e, c=c: e.tensor_copy(out=PRE[:, 0:3], in_=CARRY[:, c, :]), [bCARRY], [bPRE])
                            for (c0, n) in pieces:
                                bank, bbank = next_bank()

                                def mm(e, slot=slot, bank=bank, c0=c0, n=n):
                                    for k in range(8):
                                        ins = e.matmul(bank[:, 0:n], lhsT=slot[:, k, :], rhs=XT[:, k, c0:c0 + n],
                                                       start=(k == 0), stop=(k == 7))
                                    return ins
                                P(mm, [bslot] + xt_bufs(c0, n), [bbank])
                                if c0 < Tpg:
                                    A(lambda e, bank=bank, c0=c0, n=n, bcol=bcol: e.activation(
                                        out=PRE[:, 3 + c0:3 + c0 + n], in_=bank[:, 0:n], func=AF.Identity, bias=bcol),
                                      [bbank, bsmall], [bPRE])
                                else:
                                    A(lambda e, bank=bank, bcol=bcol: e.activation(
                                        out=PRES[:, :, 3:7], in_=bank[:, 0:64].rearrange("p (s t) -> p s t", t=4),
                                        func=AF.Identity, bias=bcol), [bbank, bsmall], [bPRES])
                            V(lambda e, c=c: e.tensor_copy(out=CARRY[:, c, :], in_=PRE[:, Tpg:Tpg + 3]),
                              [bPRE], [bCARRY])
                            yield
                            V(lambda e, cw=cw: e.tensor_scalar(out=acc[:, 0:Tpg], in0=PRE[:, 0:Tpg], scalar1=cw[0],
                                                               scalar2=None, op0=ALU.mult), [bPRE, bsmall], [bacc])
                            for i in range(1, 4):
                                V(lambda e, cw=cw, i=i: e.scalar_tensor_tensor(
                                    out=acc[:, 0:Tpg], in0=PRE[:, i:i + Tpg], scalar=cw[i], in1=acc[:, 0:Tpg],
                                    op0=ALU.mult, op1=ALU.add), [bPRE, bsmall, bacc], [bacc])
                            if sam:
                                V(lambda e, c=c: e.tensor_copy(
                                    out=PRES[:, :, 0:3], in_=SCT[:, c, :].rearrange("p (s r) -> p s r", r=3)),
                                  [bSCT], [bPRES])
                                V(lambda e, c=c: e.tensor_copy(
                                    out=NCS[:, c, :].rearrange("p (s r) -> p s r", r=3), in_=PRES[:, :, 4:7]),
                                  [bPRES], [bNCS])
                                accs = acc[:, Tpg:Tpg + 64].rearrange("p (s t) -> p s t", t=4)
                                V(lambda e, cw=cw, accs=accs: e.tensor_scalar(
                                    out=accs, in0=PRES[:, :, 0:4], scalar1=cw[0], scalar2=None, op0=ALU.mult),
                                  [bPRES, bsmall], [bacc])
                                for i in range(1, 4):
                                    V(lambda e, cw=cw, i=i, accs=accs: e.scalar_tensor_tensor(
                                        out=accs, in0=PRES[:, :, i:i + 4], scalar=cw[i], in1=accs,
                                        op0=ALU.mult, op1=ALU.add), [bPRES, bsmall, bacc], [bacc])
                            h = c % 8
                            if c >= 16:
                                A(lambda e, h=h: e.activation(out=VT[:, h, 0:Tg], in_=acc[:, 0:Tg], func=AF.Silu),
                                  [bacc], [bVT])
                                yield
                            else:
                                A(lambda e: e.activation(out=ex[:, 0:Tg], in_=acc[:, 0:Tg], func=AF.Silu),
                                  [bacc], [bex])
                                A(lambda e: e.activation(out=acc[:, 0:Tg], in_=ex[:, 0:Tg], func=AF.Square),
                                  [bex], [bacc])
                                yield
                                dst, bdst = (QT, bQT) if c < 8 else (KT, bKT)
                                qb = (-0.5 * math.log(128.0)) if c < 8 else 0.0
                                rr, brr = rrs[c % 4], brrs[c % 4]
                                for pi_, (c0, n) in enumerate(_pieces(Tg)):
                                    pso, bpso = [(PS6, bPS6), (PS5, bPS5), (PS7, bPS7)][(2 * (c % 2) + pi_) % 3]
                                    P(lambda e, c0=c0, n=n, pso=pso: e.matmul(pso[:, 0:n], lhsT=onesf[:],
                                                                              rhs=acc[:, c0:c0 + n],
                                                                              start=True, stop=True),
                                      [bacc, bconst], [bpso])
                                    A(lambda e, c0=c0, n=n, pso=pso: e.activation(out=rr[:, c0:c0 + n], in_=pso[:, 0:n],
                                                                                  func=AF.Ln, bias=RMS_EPS),
                                      [bpso], [brr])
                                    A(lambda e, c0=c0, n=n, qb=qb: e.activation(out=rr[:, c0:c0 + n],
                                                                                in_=rr[:, c0:c0 + n], func=AF.Exp,
                                                                                scale=-0.5, bias=qb), [brr], [brr])
                                yield
                                V(lambda e, dst=dst, h=h: e.tensor_tensor(
                                    out=dst[:, h, 0:Tg], in0=ex[:, 0:Tg], in1=rr[:, 0:Tg], op=ALU.mult),
                                  [brr, bex], [bdst])

                        qgens = [qkv_block(c) for c in range(24)]

                        def qstep(c):
                            next(qgens[c])

                        def qfin(c):
                            for _ in qgens[c]:
                                pass
                        for k in range(0, 15):
                            if k < 12:
                                qstep(2 * k)
                                qstep(2 * k + 1)
                            if 3 <= k <= 14:
                                qfin(2 * k - 6)
                                qfin(2 * k - 5)
                            if 1 <= k <= 12:
                                qstep(2 * k - 2)
                                qstep(2 * k - 1)
                            if 2 <= k <= 13:
                                for c_ in (2 * k - 4, 2 * k - 3):
                                    if c_ < 16:
                                        qstep(c_)

                        def epi_z(bi, c0, n, bank, bbank):
                            A(lambda e: e.activation(out=ZT[:, bi, c0:c0 + n], in_=bank[:, 0:n], func=AF.Silu,
                                                     bias=small[:, SM_BFM + 32 + bi:SM_BFM + 33 + bi]),
                              [bbank, bsmall], [bZT])
                        gemm_fm([w_in[32 + c] for c in range(8)], XT, xt_bufs,
                                pieces, epi_z)

                        if last:
                            stg = SCin if sam else sb("stg", [48, 3072], F32, sc)
                            bstg = Buf("stg")
                            S.barrier()
                            for c in range(24):
                                P(lambda e, c=c: e.transpose(out=PS5[0:3, 0:128], in_=CARRY[:, c, :],
                                                             identity=id_f(128)), [bCARRY, bmasks], [bPS5])
                                V(lambda e, c=c: e.tensor_copy(out=stg[0:3, c * 128:(c + 1) * 128],
                                                               in_=PS5[0:3, 0:128]), [bPS5], [bstg])
                            S.dma("sp", lambda e: e.dma_start(out=ncp, in_=stg[0:3, :]), reads=[bstg], writes=[Buf()])
                            if sam:
                                S.barrier()
                                for c in range(24):
                                    P(lambda e, c=c: e.transpose(out=PS5[0:48, 0:128], in_=NCS[:, c, :],
                                                                 identity=id_f(128)), [bNCS, bmasks], [bPS5])
                                    V(lambda e, c=c: e.tensor_copy(out=stg[0:48, c * 128:(c + 1) * 128],
                                                                   in_=PS5[0:48, 0:128]), [bPS5], [bstg])
                                S.dma("sp", lambda e: e.dma_start(out=ncs, in_=stg[0:48, :]), reads=[bstg],
                                      writes=[Buf()])
                        S.barrier()

                    if stop_after == "bb1":
                        return
                    with ExitStack() as sc:
                        def f32t(name):
                            return sb(name, [128, 8, 64], F32, sc)

                        def b16t(name, w=64):
                            return sb(name, [128, 8, w], BF16, sc)
                        units = []
                        if sam:
                            units.append((Tpg, True, 1, gsz))
                        for t_ in range(gsz):
                            units.append((t_ * 128, False, 2, t_))
                        NT_ = gsz + (1 if sam else 0)
                        gcs = sb("gcs", [128, 8], F32, sc)
                        gb_ = sb("gb_", [128, 8, 128], F32, sc)
                        Dm = f32t("Dm")
                        Ds = f32t("Ds")
                        Xm = f32t("Xm")
                        Tt = f32t("Tt")
                        Xb = b16t("Xb")
                        Wb = b16t("Wb")
                        Pw1 = b16t("Pw1")
                        Px1 = b16t("Px1")
                        cls = [sb("cl", [128, 8], F32, sc) for _ in range(2)]
                        EGs = [sb("EG", [128, 2, 8, 64], F32, sc) for _ in range(2)]
                        kgs = [sb("kg", [128, 2, 8, 64], BF16, sc) for _ in range(2)]
                        qgs = [sb("qg", [128, 2, 8, 64], BF16, sc) for _ in range(2)]
                        qkms = [b16t("qkm") for _ in range(2)]
                        Ttbs = [b16t("Ttb") for _ in range(2)]
                        kds = [b16t("kd", 128) for _ in range(2)]
                        vtms = [b16t("vtm", 128) for _ in range(2)]
                        smA = sb("smA", [128, 8, NT_, 8], F32, sc)
                        bsmA = Buf("smA")
                        bdA = sb("bdA", [128, NT_, 16], F32, sc)
                        bbdA = Buf("bdA")
                        vd = b16t("vd", 128)
                        vnew = b16t("vnew", 128)
                        osq = sb("osq", [128, 512], F32, sc)
                        orr = sb("orr", [128, 512], F32, sc)
                        ocp = sb("ocp", [128, 512], F32, sc)
                        bb = {n: Buf(n) for n in ("gcs", "gb", "Dm", "Ds", "Xm", "Tt", "Xb", "Wb", "Pw1", "Px1",
                                                  "vd", "vnew", "osq", "orr", "ocp")}
                        b2 = {n: [Buf(n + "0"), Buf(n + "1")] for n in ("cl", "EG", "kg", "qg", "qkm", "Ttb", "kd",
                                                                         "vtm")}
                        if sam:
                            Ssbs = [sb("Ssb", [128, 8, 128], F32, sc) for _ in range(2)]
                            Sb16s = [sb("Sb16", [128, 8, 128], BF16, sc) for _ in range(2)]
                            kdms = [sb("kdm", [64, 8, 128], BF16, sc) for _ in range(2)]
                            vdT = sb("vdT", [128, 8, 64], BF16, sc)
                            oS = sb("oS", [128, 512], F32, sc)
                            for n in ("vdT", "oS"):
                                bb[n] = Buf(n)
                            for n in ("Ssb", "Sb16", "kdm"):
                                b2[n] = [Buf(n + "0"), Buf(n + "1")]
                        pA, bpA = PB[0], bPB[0]
                        pB, bpB = PB[1], bPB[1]
                        pB2, bpB2 = PB[2], bPB[2]
                        pC, bpC = PB[3], bPB[3]
                        c1, bc1 = PS5, bPS5
                        c2, bc2 = PS6, bPS6
                        c3, bc3 = PS7, bPS7

                        def v3(ap, r0, r1):
                            return ap[r0:r1, :].rearrange("p (h i) -> p h i", h=8)

                        def v4(ap, r0, r1):
                            return ap[r0:r1, :].rearrange("p (h d) -> p h d", h=4)

                        def tp(hf):
                            return (64, 64) if hf else None

                        def tpk(hf):
                            return (0, 64) if hf else None

                        def mmx(e, out, lhsT, rhs, tpos, start=True, stop=True):
                            if tpos is None:
                                return e.matmul(out, lhsT=lhsT, rhs=rhs, start=start, stop=stop)
                            return e.matmul(out, lhsT=lhsT, rhs=rhs, start=start, stop=stop, tile_position=tpos)

                        for (ucol, uis, unh, ut) in units:
                            def mmbd(e, ucol=ucol, unh=unh, ut=ut):
                                m_rows = 64 * unh
                                for k in range(8):
                                    ins = e.matmul(PS5[0:m_rows, ut * 16:(ut + 1) * 16],
                                                   lhsT=XT[:, k, ucol:ucol + m_rows], rhs=Wbd[:, k, :],
                                                   start=(k == 0), stop=(k == 7))
                                return ins
                            P(mmbd, xt_bufs(ucol, 64 * unh) + [bWbd], [bPS5])
                        if sam:
                            V(lambda e: e.memset(bdA[64:128, gsz, :], 0.0), [], [bbdA])
                        for (ucol, uis, unh, ut) in units:
                            V(lambda e, unh=unh, ut=ut: e.tensor_tensor(
                                out=bdA[0:64 * unh, ut, :], in0=PS5[0:64 * unh, ut * 16:(ut + 1) * 16],
                                in1=small[0:64 * unh, SM_BBD:SM_BBD + 16], op=ALU.add), [bPS5, bsmall], [bbdA])
                        r_ = [smA[:, i, :, :] for i in range(8)]
                        e_, beta_, nbeta_, xd_, m_, na_, gl_ = r_[0], r_[1], r_[2], r_[3], r_[4], r_[5], r_[6]
                        A(lambda e: e.activation(out=e_, in_=bdA[:, :, 0:8], func=AF.Exp, scale=-1.0), [bbdA], [bsmA])
                        V(lambda e: e.tensor_tensor(
                            out=xd_, in0=bdA[:, :, 8:16],
                            in1=small[:, SM_DTB:SM_DTB + 8].unsqueeze(1).to_broadcast([128, NT_, 8]), op=ALU.add),
                          [bbdA, bsmall], [bsmA])
                        V(lambda e: e.tensor_scalar_max(out=m_, in0=xd_, scalar1=0.0), [bsmA], [bsmA])
                        V(lambda e: e.scalar_tensor_tensor(out=na_, in0=xd_, scalar=0.0, in1=m_, op0=ALU.min,
                                                           op1=ALU.subtract), [bsmA], [bsmA])
                        A(lambda e: e.activation(out=na_, in_=na_, func=AF.Exp), [bsmA], [bsmA])
                        V(lambda e: e.tensor_scalar_add(out=e_, in0=e_, scalar1=1.0), [bsmA], [bsmA])
                        A(lambda e: e.activation(out=na_, in_=na_, func=AF.Ln, bias=1.0), [bsmA], [bsmA])
                        V(lambda e: e.reciprocal(out=beta_, in_=e_), [bsmA], [bsmA])
                        V(lambda e: e.tensor_scalar_mul(out=nbeta_, in0=beta_, scalar1=-1.0), [bsmA], [bsmA])
                        V(lambda e: e.tensor_tensor(out=gl_, in0=na_, in1=m_, op=ALU.add), [bsmA], [bsmA])
                        V(lambda e: e.tensor_tensor(
                            out=gl_, in0=gl_, in1=nea[:, :].unsqueeze(1).to_broadcast([128, NT_, 8]), op=ALU.mult),
                          [bsmA, bconst], [bsmA])

                        def pre(ucol, is_s, nh, ut, p):
                            R1 = 64 * nh
                            mo = lambda a, b_: masks[0:R1, (a if is_s else b_):(a if is_s else b_) + 64]
                            cau = mo(MK_CAUS, MK_CAUP)
                            stri = mo(MK_STRS, MK_STRP)
                            suf = mo(MK_SUFS, MK_SUFP)
                            id2 = masks[0:R1, MK_ID2:MK_ID2 + 64]
                            EG, kg, qg, qkm, Ttb, kd, vtm = (EGs[p], kgs[p], qgs[p], qkms[p], Ttbs[p],
                                                            kds[p], vtms[p])
                            bEG, bkg, bqg, bqkm, bTtb, bkd, bvtm = (b2["EG"][p], b2["kg"][p],
                                                                    b2["qg"][p], b2["qkm"][p], b2["Ttb"][p],
                                                                    b2["kd"][p], b2["vtm"][p])
                            nbeta, gl = smA[0:R1, 2, ut, :], smA[0:R1, 6, ut, :]
                            cl, bcl = cls[p][0:R1, :], b2["cl"][p]
                            hv = list(range(nh))
                            rs = lambda hf: slice(64 * hf, 64 * hf + 64)
                            cs = lambda hf: slice(ucol + 64 * hf, ucol + 64 * hf + 64)
                            gbank = [(pB, bpB), (pB2, bpB2)]
                            V(lambda e: e.tensor_copy(out=gb_[0:R1], in_=gl.unsqueeze(2).to_broadcast([R1, 8, 128])),
                              [bsmA], [bb["gb"]])
                            yield

                            def mmg(e):
                                for hf in hv:
                                    mmx(e, pA[rs(hf), 16:24], cau[rs(hf), :], smA[rs(hf), 6, ut, :], tp(hf))
                                    ins = mmx(e, pA[rs(hf), 24:32], suf[rs(hf), :], smA[rs(hf), 6, ut, :], tp(hf))
                                return ins
                            P(mmg, [bsmA, bmasks], [bpA])

                            def mmgb(e):
                                for hf in hv:
                                    for h in range(8):
                                        ins = e.matmul(gbank[hf][0][:, h * 64:(h + 1) * 64], lhsT=gb_[rs(hf), h, :],
                                                       rhs=cau[rs(hf), :], start=True, stop=True)
                                return ins
                            P(mmgb, [bb["gb"], bmasks], [bpB, bpB2])
                            yield
                            V(lambda e: e.tensor_copy(out=gcs[0:R1], in_=pA[0:R1, 16:24]), [bpA], [bb["gcs"]])
                            for hf in hv:
                                A(lambda e, hf=hf: e.activation(out=EG[:, hf], in_=v3(gbank[hf][0], 0, 128), func=AF.Exp),
                                  [gbank[hf][1]], [bEG])
                            yield
                            A(lambda e: e.activation(out=cl, in_=pA[0:R1, 24:32], func=AF.Exp), [bpA], [bcl])
                            for hf in hv:
                                V(lambda e, hf=hf: e.tensor_tensor(
                                    out=Dm[rs(hf)], in0=v3(gbank[hf][0], 64 * hf, 64 * hf + 64),
                                    in1=gcs[rs(hf)].unsqueeze(2).to_broadcast([64, 8, 64]),
                                    op=ALU.subtract), [gbank[hf][1], bb["gcs"]], [bb["Dm"]])
                            yield

                            def trk(e):
                                for hf in hv:
                                    for h in range(8):
                                        if hf:
                                            ins = e.transpose(out=PT[rs(hf), h, :], in_=KT[:, h, cs(hf)],
                                                              identity=identb[:], tile_position=(0, 64))
                                        else:
                                            ins = e.transpose(out=PT[rs(hf), h, :], in_=KT[:, h, cs(hf)],
                                                              identity=identb[:])
                                return ins
                            P(trk, [bKT, bconst], [bPT])
                            yield
                            V(lambda e: e.tensor_tensor(out=kd[0:R1], in0=PT[0:R1, :, :],
                                                        in1=cl.unsqueeze(2).to_broadcast([R1, 8, 128]), op=ALU.mult),
                              [bPT, bcl], [bkd])
                            yield
                            if not is_s:
                                def trv(e):
                                    for hf in hv:
                                        for h in range(8):
                                            if hf:
                                                ins = e.transpose(out=PT[rs(hf), h, :], in_=VT[:, h, cs(hf)],
                                                                  identity=identb[:], tile_position=(0, 64))
                                            else:
                                                ins = e.transpose(out=PT[rs(hf), h, :], in_=VT[:, h, cs(hf)],
                                                                  identity=identb[:])
                                    return ins
                                P(trv, [bVT, bconst], [bPT])
                                yield
                                A(lambda e: e.activation(out=vtm[0:R1], in_=PT[0:R1, :, :], func=AF.Copy),
                                  [bPT], [bvtm])
                                yield

                            def mmkk(e):
                                for hf in hv:
                                    for h in range(8):
                                        mmx(e, pA[rs(hf), h * 64:(h + 1) * 64], KT[:, h, cs(hf)], KT[:, h, cs(hf)],
                                            tpk(hf))
                                        ins = mmx(e, pB[rs(hf), h * 64:(h + 1) * 64], KT[:, h, cs(hf)],
                                                  QT[:, h, cs(hf)], tpk(hf))
                                return ins
                            P(mmkk, [bKT, bQT], [bpA, bpB])
                            V(lambda e: e.tensor_tensor(out=Dm[0:R1], in0=Dm[0:R1],
                                                        in1=cau.unsqueeze(1).to_broadcast([R1, 8, 64]), op=ALU.mult),
                              [bb["Dm"], bmasks], [bb["Dm"]])
                            yield
                            A(lambda e: e.activation(out=Dm[0:R1], in_=Dm[0:R1], func=AF.Exp), [bb["Dm"]], [bb["Dm"]])
                            for hf in hv:
                                G(lambda e, hf=hf: e.tensor_tensor(out=kg[:, hf], in0=KT[:, :, cs(hf)], in1=EG[:, hf],
                                                                   op=ALU.mult), [bKT, bEG], [bkg])
                            yield
                            G(lambda e: e.tensor_tensor(out=Ds[0:R1], in0=Dm[0:R1],
                                                        in1=stri.unsqueeze(1).to_broadcast([R1, 8, 64]), op=ALU.mult),
                              [bb["Dm"], bmasks], [bb["Ds"]])
                            G(lambda e: e.tensor_tensor(out=Dm[0:R1], in0=Dm[0:R1],
                                                        in1=cau.unsqueeze(1).to_broadcast([R1, 8, 64]), op=ALU.mult),
                              [bb["Dm"], bmasks], [bb["Dm"]])
                            yield
                            V(lambda e: e.tensor_tensor(out=Xm[0:R1], in0=v3(pA, 0, R1), in1=Ds[0:R1], op=ALU.mult),
                              [bpA, bb["Ds"]], [bb["Xm"]])
                            V(lambda e: e.tensor_tensor(out=Xm[0:R1], in0=Xm[0:R1],
                                                        in1=nbeta.unsqueeze(2).to_broadcast([R1, 8, 64]),
                                                        op=ALU.mult), [bb["Xm"], bsmA], [bb["Xm"]])
                            yield
                            A(lambda e: e.activation(out=Xb[0:R1], in_=Xm[0:R1], func=AF.Copy), [bb["Xm"]], [bb["Xb"]])
                            V(lambda e: e.tensor_tensor(out=qkm[0:R1], in0=v3(pB, 0, R1), in1=Dm[0:R1], op=ALU.mult),
                              [bpB, bb["Dm"]], [bqkm])
                            yield

                            def trx(e):
                                for hf in hv:
                                    for h in range(8):
                                        if hf:
                                            ins = e.transpose(out=PT[rs(hf), h, 0:64], in_=Xb[rs(hf), h, :],
                                                              identity=identb[64:128, 64:128], tile_position=(64, 64))
                                        else:
                                            ins = e.transpose(out=PT[rs(hf), h, 0:64], in_=Xb[rs(hf), h, :],
                                                              identity=identb[0:64, 0:64])
                                return ins
                            P(trx, [bb["Xb"], bconst], [bPT])
                            G(lambda e: e.tensor_tensor(
                                out=Tt[0:R1], in0=Xm[0:R1], in1=id2.unsqueeze(1).to_broadcast([R1, 8, 64]),
                                op=ALU.add), [bb["Xm"], bmasks], [bb["Tt"]])
                            yield
                            V(lambda e: e.tensor_copy(out=Wb[0:R1], in_=PT[0:R1, :, 0:64]), [bPT], [bb["Wb"]])
                            A(lambda e: e.activation(out=Ttb[0:R1], in_=Tt[0:R1], func=AF.Copy), [bb["Tt"]], [bTtb])
                            yield
                            for hf in hv:
                                G(lambda e, hf=hf: e.tensor_tensor(out=qg[:, hf], in0=QT[:, :, cs(hf)], in1=EG[:, hf],
                                                                   op=ALU.mult), [bQT, bEG], [bqg])
                            nlev = 2 if is_s else 5
                            alt = [(Pw1, Px1, bb["Pw1"], bb["Px1"]), (Wb, Xb, bb["Wb"], bb["Xb"])]

                            def do_mmsq(Pw, Px, bPw, bPx, lastl):
                                def mmsq(e):
                                    for hf in hv:
                                        for h in range(8):
                                            ins = mmx(e, pA[rs(hf), h * 64:(h + 1) * 64], Px[rs(hf), h, :],
                                                      Pw[rs(hf), h, :], tp(hf))
                                            if not lastl:
                                                ins = mmx(e, pB[rs(hf), h * 64:(h + 1) * 64], Pw[rs(hf), h, :],
                                                          Px[rs(hf), h, :], tp(hf))
                                    return ins
                                P(mmsq, [bPw, bPx], [bpA, bpB])

                            def do_copies(nPw, nPx, bnPw, bnPx, lastl):
                                A(lambda e: e.activation(out=nPw[0:R1], in_=v3(pA, 0, R1), func=AF.Copy),
                                  [bpA], [bnPw])
                                if not lastl:
                                    V(lambda e: e.tensor_copy(out=nPx[0:R1], in_=v3(pB, 0, R1)), [bpB], [bnPx])

                            def do_mminc(nPw, bnPw):
                                def mminc(e):
                                    for hf in hv:
                                        idb = identb[64:128, 64:128] if hf else identb[0:64, 0:64]
                                        for h in range(8):
                                            mmx(e, pC[rs(hf), h * 64:(h + 1) * 64], idb, Ttb[rs(hf), h, :], tp(hf),
                                                start=True, stop=False)
                                            ins = mmx(e, pC[rs(hf), h * 64:(h + 1) * 64], nPw[rs(hf), h, :],
                                                      Ttb[rs(hf), h, :], tp(hf), start=False, stop=True)
                                    return ins
                                P(mminc, [bnPw, bTtb, bconst], [bpC])

                            cur = (Wb, Xb, bb["Wb"], bb["Xb"])
                            do_mmsq(cur[0], cur[1], cur[2], cur[3], nlev == 1)
                            yield
                            nxt_ = alt[0]
                            do_copies(nxt_[0], nxt_[1], nxt_[2], nxt_[3], nlev == 1)
                            yield
                            for lev in range(1, nlev + 1):
                                lvl = alt[(lev - 1) % 2]
                                if lev < nlev:
                                    do_mmsq(lvl[0], lvl[1], lvl[2], lvl[3], lev + 1 == nlev)
                                do_mminc(lvl[0], lvl[2])
                                yield
                                if lev < nlev:
                                    nx2 = alt[lev % 2]
                                    do_copies(nx2[0], nx2[1], nx2[2], nx2[3], lev + 1 == nlev)
                                A(lambda e: e.activation(out=Ttb[0:R1], in_=v3(pC, 0, R1), func=AF.Copy), [bpC], [bTtb])
                                yield

                        def chain(ucol, is_s, hf, ut, p):
                            cs0 = ucol + 64 * hf
                            cs1 = cs0 + 64
                            r0, r1 = 64 * hf, 64 * hf + 64
                            EG, kg, qg = EGs[p][:, hf], kgs[p][:, hf], qgs[p][:, hf]
                            qkm, Ttb, kd, vtm = qkms[p], Ttbs[p], kds[p], vtms[p]
                            bEG, bkg, bqg, bqkm, bTtb, bkd, bvtm = (b2["EG"][p], b2["kg"][p],
                                                                    b2["qg"][p], b2["qkm"][p], b2["Ttb"][p],
                                                                    b2["kd"][p], b2["vtm"][p])
                            beta = smA[r0:r1, 1, ut, :]
                            cb = [(c1, bc1), (c2, bc2)]

                            def tv_and_vnew():
                                def mmtv(e):
                                    for h in range(8):
                                        ins = mmx(e, cb[h // 4][0][r0:r1, (h % 4) * 128:(h % 4 + 1) * 128],
                                                  Ttb[r0:r1, h, :], vd[r0:r1, h, :], tp(hf))
                                    return ins
                                P(mmtv, [bTtb, bb["vd"]], [bc1, bc2])
                                for hh in range(2):
                                    V(lambda e, hh=hh: e.tensor_tensor(
                                        out=vnew[r0:r1, hh * 4:(hh + 1) * 4, :], in0=v4(cb[hh][0], r0, r1),
                                        in1=beta[:, hh * 4:(hh + 1) * 4].unsqueeze(2).to_broadcast([64, 4, 128]),
                                        op=ALU.mult), [cb[hh][1], bsmA], [bb["vnew"]])

                            if not is_s:
                                def mmkgs(e):
                                    for h in range(8):
                                        ins = mmx(e, cb[h // 4][0][r0:r1, (h % 4) * 128:(h % 4 + 1) * 128],
                                                  kg[:, h, :], Sbf[:, h, :], tpk(hf))
                                    return ins
                                P(mmkgs, [bkg, bSbf], [bc1, bc2])
                                yield
                                for hh in range(2):
                                    V(lambda e, hh=hh: e.tensor_tensor(
                                        out=vd[r0:r1, hh * 4:(hh + 1) * 4, :], in0=vtm[r0:r1, hh * 4:(hh + 1) * 4, :],
                                        in1=v4(cb[hh][0], r0, r1), op=ALU.subtract), [cb[hh][1], bvtm], [bb["vd"]])
                                yield
                                tv_and_vnew()
                                yield

                                def mmo(e):
                                    for h in range(8):
                                        e.matmul(c3[:, h * 64:(h + 1) * 64], lhsT=Sbf[:, h, :], rhs=qg[:, h, :],
                                                 start=True, stop=False)
                                        ins = e.matmul(c3[:, h * 64:(h + 1) * 64], lhsT=vnew[r0:r1, h, :],
                                                       rhs=qkm[r0:r1, h, :], start=False, stop=True)
                                    return ins
                                P(mmo, [bSbf, bqg, bb["vnew"], bqkm], [bc3])

                                def mmsi(e):
                                    for h in range(8):
                                        ins = e.matmul(cb[h // 4][0][:, (h % 4) * 128:(h % 4 + 1) * 128],
                                                       lhsT=kd[r0:r1, h, :], rhs=vnew[r0:r1, h, :],
                                                       start=True, stop=True)
                                    return ins
                                P(mmsi, [bkd, bb["vnew"]], [bc1, bc2])
                                yield
                                for hh in range(2):
                                    G(lambda e, hh=hh: e.tensor_tensor(
                                        out=Sst[:, hh * 4:(hh + 1) * 4, :], in0=Sst[:, hh * 4:(hh + 1) * 4, :],
                                        in1=EG[:, hh * 4:(hh + 1) * 4, 63:64].to_broadcast([128, 4, 128]),
                                        op=ALU.mult), [bSst, bEG], [bSst])
                                    V(lambda e, hh=hh: e.tensor_tensor(
                                        out=Sst[:, hh * 4:(hh + 1) * 4, :], in0=Sst[:, hh * 4:(hh + 1) * 4, :],
                                        in1=v4(cb[hh][0], 0, 128), op=ALU.add), [bSst, cb[hh][1]], [bSst])
                                yield
                                A(lambda e: e.activation(out=Sbf[:], in_=Sst[:], func=AF.Copy), [bSst], [bSbf])
                                A(lambda e: e.activation(out=ocp[:], in_=c3[:, :], func=AF.Copy), [bc3], [bb["ocp"]])
                                osrc, bosrc = ocp, bb["ocp"]
                            else:
                                for s in range(16):
                                    Ssb, Sb16 = Ssbs[s % 2], Sb16s[s % 2]
                                    bSsb, bSb16 = b2["Ssb"][s % 2], b2["Sb16"][s % 2]
                                    S.dma("sp", lambda e, s=s, Ssb=Ssb: e.dma_start(
                                        out=Ssb[:], in_=sssm[s].rearrange("h k v -> k h v")), writes=[bSsb])
                                    A(lambda e, Ssb=Ssb, Sb16=Sb16: e.activation(out=Sb16[:], in_=Ssb[:], func=AF.Copy),
                                      [bSsb], [bSb16])

                                    def mms(e, s=s, Sb16=Sb16):
                                        for h in range(8):
                                            e.matmul(c1[:, h * 64 + 4 * s:h * 64 + 4 * s + 4], lhsT=Sb16[:, h, :],
                                                     rhs=kg[:, h, 4 * s:4 * s + 4], start=True, stop=True)
                                            ins = e.matmul(c2[:, h * 64 + 4 * s:h * 64 + 4 * s + 4],
                                                           lhsT=Sb16[:, h, :], rhs=qg[:, h, 4 * s:4 * s + 4],
                                                           start=True, stop=True)
                                        return ins
                                    P(mms, [bSb16, bkg, bqg], [bc1, bc2])
                                    yield
                                V(lambda e: e.tensor_tensor(out=vdT[:], in0=VT[:, :, cs0:cs1], in1=v3(c1, 0, 128),
                                                            op=ALU.subtract), [bVT, bc1], [bb["vdT"]])
                                A(lambda e: e.activation(out=oS[:], in_=c2[:, :], func=AF.Copy),
                                  [bc2], [bb["oS"]])
                                yield

                                def trvd(e):
                                    for h in range(8):
                                        ins = e.transpose(out=PT[0:64, h, :], in_=vdT[:, h, :], identity=identb[:])
                                    return ins
                                P(trvd, [bb["vdT"], bconst], [bPT])
                                yield
                                A(lambda e: e.activation(out=vd[0:64], in_=PT[0:64, :, :], func=AF.Copy),
                                  [bPT], [bb["vd"]])
                                yield
                                tv_and_vnew()
                                yield

                                def mmo(e):
                                    for h in range(8):
                                        ins = e.matmul(c3[:, h * 64:(h + 1) * 64], lhsT=vnew[0:64, h, :],
                                                       rhs=qkm[0:64, h, :], start=True, stop=True)
                                    return ins
                                P(mmo, [bb["vnew"], bqkm], [bc3])
                                yield
                                V(lambda e: e.tensor_tensor(out=oS[:], in0=oS[:], in1=c3[:, :], op=ALU.add),
                                  [bc3, bb["oS"]], [bb["oS"]])
                                for s in range(16):
                                    Ssb, kdm = Ssbs[s % 2], kdms[s % 2]
                                    bSsb, bkdm = b2["Ssb"][s % 2], b2["kdm"][s % 2]
                                    S.dma("sp", lambda e, s=s, Ssb=Ssb: e.dma_start(
                                        out=Ssb[:], in_=sssm[s].rearrange("h k v -> k h v")), writes=[bSsb])
                                    V(lambda e, s=s, kdm=kdm: e.tensor_scalar(
                                        out=kdm[:], in0=kd[0:64], scalar1=masks[0:64, MK_SEL + s:MK_SEL + s + 1],
                                        scalar2=None, op0=ALU.mult), [bkd, bmasks], [bkdm])

                                    def mmsi(e, kdm=kdm):
                                        for h in range(8):
                                            ins = e.matmul(cb[h // 4][0][:, (h % 4) * 128:(h % 4 + 1) * 128],
                                                           lhsT=kdm[:, h, :], rhs=vnew[0:64, h, :],
                                                           start=True, stop=True)
                                        return ins
                                    P(mmsi, [bkdm, bb["vnew"]], [bc1, bc2])
                                    for hh in range(2):
                                        V(lambda e, hh=hh, s=s, Ssb=Ssb: e.tensor_tensor(
                                            out=Ssb[:, hh * 4:(hh + 1) * 4, :], in0=Ssb[:, hh * 4:(hh + 1) * 4, :],
                                            in1=EG[:, hh * 4:(hh + 1) * 4, 4 * s + 3:4 * s + 4].to_broadcast(
                                                [128, 4, 128]), op=ALU.mult), [bSsb, bEG], [bSsb])
                                        V(lambda e, hh=hh, Ssb=Ssb: e.tensor_tensor(
                                            out=Ssb[:, hh * 4:(hh + 1) * 4, :], in0=Ssb[:, hh * 4:(hh + 1) * 4, :],
                                            in1=v4(cb[hh][0], 0, 128), op=ALU.add), [bSsb, cb[hh][1]], [bSsb])
                                    S.dma("sp", lambda e, s=s, Ssb=Ssb: e.dma_start(
                                        out=nss[s].rearrange("h k v -> k h v"), in_=Ssb[:]),
                                        reads=[bSsb], writes=[Buf()])
                                    yield
                                osrc, bosrc = oS, bb["oS"]
                            A(lambda e: e.activation(out=osq[:], in_=osrc[:, :], func=AF.Square), [bosrc], [bb["osq"]])
                            yield
                            P(lambda e: e.matmul(c3[:, :], lhsT=onesf[:], rhs=osq[:], start=True, stop=True),
                              [bb["osq"], bconst], [bc3])
                            yield
                            A(lambda e: e.activation(out=orr[:], in_=c3[:, :], func=AF.Ln, scale=1.0 / 128,
                                                     bias=RMS_EPS), [bc3], [bb["orr"]])
                            A(lambda e: e.activation(out=orr[:], in_=orr[:], func=AF.Exp, scale=-0.5),
                              [bb["orr"]], [bb["orr"]])
                            yield
                            V(lambda e: e.tensor_tensor(out=osq[:], in0=osrc[:, :], in1=orr[:], op=ALU.mult),
                              [bosrc, bb["orr"]], [bb["osq"]])
                            yield
                            V(lambda e: e.scalar_tensor_tensor(
                                out=ZT[:, :, cs0:cs1], in0=osq[:, :].rearrange("p (h i) -> p h i", h=8),
                                scalar=small[:, SM_NRM:SM_NRM + 1],
                                in1=ZT[:, :, cs0:cs1], op0=ALU.mult, op1=ALU.mult),
                              [bb["osq"], bsmall, bZT], [bZT])
                            yield

                        def run_all(g):
                            for _ in g:
                                pass

                        def interleave(gens):
                            gens = [g for g in gens if g is not None]
                            while gens:
                                for g in list(gens):
                                    try:
                                        next(g)
                                    except StopIteration:
                                        gens.remove(g)

                        def chain_unit(u, p):
                            ucol, uis, unh, ut = u
                            for hf in range(unh):
                                for _ in chain(ucol, uis, hf, ut, p):
                                    yield

                        run_all(pre(units[0][0], units[0][1], units[0][2], units[0][3], 0))
                        for ui, u in enumerate(units):
                            nxt = None
                            if ui + 1 < len(units):
                                un = units[ui + 1]
                                nxt = pre(un[0], un[1], un[2], un[3], (ui + 1) % 2)
                            if OVERLAP:
                                interleave([nxt, chain_unit(u, ui % 2)])
                            else:
                                run_all(chain_unit(u, ui % 2))
                                if nxt is not None:
                                    run_all(nxt)
                        if last:
                            S.dma("sp", lambda e: e.dma_start(out=nsp.rearrange("h k v -> k h v"), in_=Sst[:]),
                                  reads=[bSst], writes=[Buf()])
                        S.barrier()

                if stop_after == "bb2":
                    return
                with ExitStack() as sc:
                    GBT = sb("GBT", [128, 8, TG], F32, sc)
                    bGBT = Buf("GBT")
                    MTb = sb("MTb", [128, 8, TG], BF16, sc)
                    bMTb = Buf("MTb")
                    WsmC = sb("Wsm2", [128, 8, D], BF16, sc)
                    bWsmC = Buf("Wsm2")
                    t0C = sb("t0c", [128, 512], F32, sc)
                    bt0C = Buf("t0c")
                    S.dma("pool", lambda e: e.dma_start(out=WsmC[:], in_=w_o),
                          writes=[bWsmC], ring="wr", nring=8)
                    load_gb(2, 3)

                    def epi_gb(bi, c0, n, bank, bbank):
                        A(lambda e: e.activation(out=GBT[:, bi, c0:c0 + n], in_=bank[:, 0:n], func=AF.Sigmoid,
                                                 bias=small[:, SM_BFM + 48 + bi:SM_BFM + 49 + bi]),
                          [bbank, bsmall], [bGBT])
                    gemm_fm([w_in[48 + c] for c in range(8)], XT, xt_bufs,
                            pieces, epi_gb)

                    def epi_bb(bi, c0, n, bank, bbank):
                        V(lambda e: e.tensor_tensor(out=t0C[:, 0:n], in0=GBT[:, bi, c0:c0 + n], in1=bank[:, 0:n],
                                                    op=ALU.mult), [bbank, bGBT], [bt0C])
                        V(lambda e: e.tensor_tensor(out=MTb[:, bi, c0:c0 + n], in0=MT[:, bi, c0:c0 + n],
                                                    in1=t0C[:, 0:n], op=ALU.add), [bt0C, bMT], [bMTb])
                    gemm_fm([w_bb[c] for c in range(8)], ZT, lambda c0, n: [bZT],
                            pieces, epi_bb)
                    def wo_tile(li, col0, pt):
                        for half in range(2):
                            bank, bbank = next_bank()

                            def mmo2(e, bank=bank, col0=col0, pt=pt, half=half):
                                for k in range(8):
                                    ins = e.matmul(bank[0:pt, :], lhsT=MTb[:, k, col0:col0 + pt],
                                                   rhs=WsmC[:, k, half * 512:(half + 1) * 512],
                                                   start=(k == 0), stop=(k == 7))
                                return ins
                            P(mmo2, [bMTb, bWsmC], [bbank])
                            V(lambda e, bank=bank, li=li, pt=pt, half=half: e.scalar_tensor_tensor(
                                out=R[0:pt, li, half * 512:(half + 1) * 512],
                                in0=R[0:pt, li, half * 512:(half + 1) * 512], scalar=ALPHA, in1=bank[0:pt, :],
                                op0=ALU.mult, op1=ALU.add), [bbank, bR[li]], [bR[li]])

                    def fin2_tile(li, col0, pt):
                        layer_norm([(li, col0, pt)], lambda li_, pt_: R[0:pt_, li_, :],
                                   lambda li_, pt_: R[0:pt_, li_, :], bR, bR)
                        to_xt(li, col0, pt)
                    prev = None
                    for t_ in tl:
                        wo_tile(*t_)
                        if prev is not None:
                            fin2_tile(*prev)
                        prev = t_
                    fin2_tile(*prev)
                    S.barrier()

            ffn(1, final=True)

        tb_ = 0
        for gidx_, gsz_ in enumerate(groups):
            do_group(gidx_, gsz_, tb_)
            tb_ += gsz_

        S.barrier(final=True)
        S.emit()
        build.last_sched = S
    return nc


def _masks():
    m = np.zeros((128, NMK), np.float32)
    m[:, MK_ID:MK_ID + 128] = np.eye(128, dtype=np.float32)
    s = np.arange(128)[:, None]
    t = np.arange(128)[None, :]
    m[:, MK_GMP:MK_GMP + 128] = (s <= t)
    j = np.arange(64)[:, None]
    i = np.arange(64)[None, :]
    same = (j // 4) == (i // 4)
    m[0:64, MK_GMS:MK_GMS + 64] = (j <= i) & same
    m[0:64, MK_CAUP:MK_CAUP + 64] = (i >= j)
    m[0:64, MK_STRP:MK_STRP + 64] = (i > j)
    m[0:64, MK_CAUS:MK_CAUS + 64] = (i >= j) & same
    m[0:64, MK_STRS:MK_STRS + 64] = (i > j) & same
    m[0:64, MK_SUFP:MK_SUFP + 64] = (j > i)
    m[0:64, MK_SUFS:MK_SUFS + 64] = (j > i) & same
    m[0:64, MK_SEL:MK_SEL + 16] = (np.arange(64)[:, None] // 4) == np.arange(16)[None, :]
    m[0:64, MK_ID2:MK_ID2 + 64] = np.eye(64, dtype=np.float32)
    for c0 in (MK_CAUP, MK_STRP, MK_SUFP, MK_ID2):
        m[64:128, c0:c0 + 64] = m[0:64, c0:c0 + 64]
    return m


def _blocks(w, c0, nb):
    w = np.asarray(w, np.float32)[:, c0:c0 + nb * 128]
    return np.ascontiguousarray(w.reshape(8, 128, nb, 128).transpose(2, 1, 0, 3))


def _up_blocks(w):
    return np.concatenate([_blocks(w, 0, NJ), _blocks(w, DFF, NJ)])


def _kmajor(w):
    w = np.asarray(w, np.float32)
    return np.ascontiguousarray(w.reshape(8, 128, w.shape[1]).transpose(1, 0, 2))


def prep_shared(p):
    f = np.float32
    b_in = np.asarray(p["b_in"], f)
    small = np.zeros((128, NSM), f)

    def fm(v):
        return np.ascontiguousarray(v.reshape(-1, 128).T)
    small[:, 0:8] = fm(b_in[OFF_U:OFF_U + 1024])
    small[:, 8:32] = fm(b_in[OFF_Q:OFF_Q + 3072])
    small[:, 32:40] = fm(b_in[OFF_Z:OFF_Z + 1024])
    small[:, 40:48] = fm(b_in[OFF_GA:OFF_GA + 1024])
    small[:, 48:56] = fm(b_in[OFF_GB:OFF_GB + 1024])
    cw = np.asarray(p["dn_conv_w"], f)
    small[:, SM_CONV:SM_CONV + 96] = cw.reshape(4, 24, 128).transpose(2, 1, 0).reshape(128, 96)
    small[:, SM_NRM] = np.asarray(p["dn_norm_w"], f)
    small[:, SM_BBD:SM_BBD + 16] = b_in[OFF_BD:OFF_BD + 16][None, :]
    small[:, SM_ALOG:SM_ALOG + 8] = np.asarray(p["dn_a_log"], f)[None, :]
    small[:, SM_DTB:SM_DTB + 8] = np.asarray(p["dn_dt_bias"], f)[None, :]
    vecs = [p["ln1_g"], p["ln1_b"], p["ln2_g"], p["ln2_b"], p["ln3_g"], p["ln3_b"], p["gm_v_g"], p["gm_v_b"],
            b_in[OFF_V:OFF_V + 1024]]
    lnbc = np.ascontiguousarray(np.broadcast_to(np.stack([np.asarray(v, f) for v in vecs])[:, None, :],
                                                (9, 128, D)))
    ws = np.asarray(p["gm_w_s"], f)
    wsT = np.ascontiguousarray(ws.transpose(2, 0, 1))
    idx = np.arange(64) % 4
    wsTs = np.ascontiguousarray(ws[:, idx][:, :, idx].transpose(2, 0, 1))
    bs = np.asarray(p["gm_b_s"], f)
    bsr = np.ascontiguousarray(bs[None, :, :])
    bsrs = np.ascontiguousarray(bs[:, idx][None, :, :])
    return {
        "w_up1": _up_blocks(p["ffn1_w_up"]), "w_dn1": np.ascontiguousarray(p["ffn1_w_down"], f),
        "w_up2": _up_blocks(p["ffn2_w_up"]), "w_dn2": np.ascontiguousarray(p["ffn2_w_down"], f),
        "w_in": np.concatenate([_blocks(p["w_in"], OFF_U, 8), _blocks(p["w_in"], OFF_Q, 24),
                                _blocks(p["w_in"], OFF_Z, 8), _blocks(p["w_in"], OFF_GA, 8),
                                _blocks(p["w_in"], OFF_GB, 8)]),
        "w_v": _kmajor(np.asarray(p["w_in"], f)[:, OFF_V:OFF_V + D]),
        "w_bd": _kmajor(np.asarray(p["w_in"], f)[:, OFF_BD:OFF_BD + 16]),
        "w_ba": _blocks(p["w_branch_a"], 0, 8), "w_bb": _blocks(p["w_branch_b"], 0, 8),
        "w_o": _kmajor(np.asarray(p["w_out"], f)),
        "lnbc": lnbc, "small": small, "masks": _masks(), "wsT": wsT, "wsTs": wsTs, "bsr": bsr, "bsrs": bsrs,
    }


PARAM_KEYS = ["ffn1_w_up", "ffn1_w_down", "ln1_g", "ln1_b", "w_in", "b_in", "gm_v_g", "gm_v_b", "gm_w_s", "gm_b_s",
              "dn_conv_w", "dn_a_log", "dn_dt_bias", "dn_norm_w", "w_branch_a", "w_branch_b", "w_out", "ln2_g",
              "ln2_b", "ffn2_w_up", "ffn2_w_down", "ln3_g", "ln3_b"]

GROUPS = [4, 4, 4, 4]
_NC_CACHE = {}


def kernel(**inputs):
    f = np.float32
    x_prompt = np.asarray(inputs["x_prompt"], f)
    x_sample = np.asarray(inputs["x_sample"], f)
    state_conv = np.asarray(inputs["state_conv"], f)[0]
    state_ssm = np.asarray(inputs["state_ssm"], f)[0]
    p = {k: np.asarray(inputs[k], f)[0] for k in PARAM_KEYS}
    shared = prep_shared(p)
    n = 8
    B, T, _ = x_prompt.shape
    assert B == n and T == 2048
    key = "full"
    if key not in _NC_CACHE:
        _NC_CACHE[key] = build(T // 128, GROUPS, has_s=True)
    nc = _NC_CACHE[key]
    in_maps = []
    for c in range(n):
        m = dict(shared)
        m["xp"] = np.ascontiguousarray(x_prompt[c])
        m["xs"] = np.ascontiguousarray(x_sample[c * 16:(c + 1) * 16].reshape(64, D))
        m["sconv"] = np.ascontiguousarray(state_conv[c * 16:(c + 1) * 16].reshape(48, 3072))
        m["sssm"] = np.ascontiguousarray(state_ssm[c * 16:(c + 1) * 16])
        in_maps.append(m)
    res = run_bass_kernel_spmd(nc, in_maps, core_ids=list(range(n)))
    r = res.results
    y_p = np.stack([r[c]["yp"] for c in range(n)]).astype(f)
    y_s = np.concatenate([r[c]["ys"].reshape(16, 4, D) for c in range(n)]).astype(f)
    c_p = np.stack([r[c]["ncp"] for c in range(n)])[None].astype(f)
    s_p = np.stack([r[c]["nsp"] for c in range(n)])[None].astype(f)
    c_s = np.concatenate([r[c]["ncs"].reshape(16, 3, 3072) for c in range(n)])[None].astype(f)
    s_s = np.concatenate([r[c]["nss"] for c in range(n)])[None].astype(f)
    v_s = np.concatenate([r[c]["ngv"].reshape(16, 4, D) for c in range(n)])[None].astype(f)
    return (y_p, y_s, c_p, s_p, c_s, s_s, v_s)
```

```python
import math
from contextlib import ExitStack

import numpy as np
import concourse.bass as bass
import concourse.mybir as mybir
from concourse.bass_utils import run_bass_kernel_spmd

import os
CSTOP = int(os.environ.get('CSTOP', '0'))
OVERLAP = int(os.environ.get('OVERLAP', '1'))
F32 = mybir.dt.float32
BF16 = mybir.dt.bfloat16
AF = mybir.ActivationFunctionType
ALU = mybir.AluOpType

D = 1024
DFF = 2816
NJ = 22
ALPHA = 2.0 ** 0.25
LN_EPS = 1e-5
RMS_EPS = 1e-6
GELU_K = math.sqrt(2.0 / math.pi)

OFF_U, OFF_V, OFF_Q, OFF_Z, OFF_BD, OFF_GA, OFF_GB = 0, 1024, 2048, 5120, 6144, 6160, 7184
SM_BFM, SM_CONV, SM_NRM, SM_BBD, SM_ALOG, SM_DTB, NSM = 0, 56, 152, 153, 169, 177, 192
MK_ID, MK_GMP, MK_GMS, MK_CAUP, MK_STRP, MK_CAUS, MK_STRS, MK_SUFP, MK_SUFS, MK_SEL, MK_ID2, NMK = (
    0, 128, 256, 320, 384, 448, 512, 576, 640, 704, 720, 784)


class Buf:
    __slots__ = ("name", "w", "r", "excl")

    def __init__(self, name="", excl=False):
        self.name = name
        self.w = None
        self.r = {}
        self.excl = excl


class Sched:
    ENGS = ("pe", "act", "dve", "pool", "sp")

    def __init__(self, nc, stack):
        self.nc = nc
        self.ops = {e: [] for e in self.ENGS}
        self.sem = {}
        self.cnt = {}
        self.known = {e: {} for e in self.ENGS}
        self.stack = stack
        for e in self.ENGS:
            self._mksem("E_" + e)
        self.rings = {}

    def _mksem(self, key):
        self.sem[key] = self.stack.enter_context(self.nc.semaphore(key))
        self.cnt[key] = 0

    def _deps(self, reads, writes):
        deps = {}

        def add(ev):
            if ev is None:
                return
            k, v = ev
            if deps.get(k, 0) < v:
                deps[k] = v
        for b in reads:
            add(b.w)
        for b in writes:
            add(b.w)
            for k, v in b.r.items():
                add((k, v))
        return deps

    def _waits(self, e, deps, self_sync):
        out = []
        kn = self.known[e]
        for k, v in deps.items():
            if k == "E_" + e and not self_sync:
                continue
            if kn.get(k, 0) >= v:
                continue
            kn[k] = v
            out.append((k, v))
        return out

    def _mark(self, ev, reads, writes):
        k, v = ev
        for b in reads:
            if b.excl:
                if b not in writes:
                    b.w = ev
                    b.r = {}
                continue
            if b.r.get(k, 0) < v:
                b.r[k] = v
        for b in writes:
            b.w = ev
            b.r = {}

    def op(self, e, fn, reads=(), writes=()):
        deps = self._deps(reads, writes)
        if e == "pool":
            self._merge_bar(deps)
        waits = self._waits(e, deps, self_sync=(e != "pe"))
        k = "E_" + e
        self.cnt[k] += 1
        ev = (k, self.cnt[k])
        self.ops[e].append((waits, fn, k, 1))
        self._mark(ev, reads, writes)
        return ev

    def dma(self, e, fn, reads=(), writes=(), ring="io", nring=16):
        if ring not in self.rings:
            keys = []
            for i in range(nring):
                key = "D_%s_%d" % (ring, i)
                self._mksem(key)
                keys.append(key)
            self.rings[ring] = [keys, 0]
        keys, idx = self.rings[ring]
        key = keys[idx % len(keys)]
        self.rings[ring][1] = idx + 1
        deps = self._deps(reads, writes)
        if e == "pool" and ring != "w":
            self._merge_bar(deps)
        if self.cnt[key] > 0 and deps.get(key, 0) < self.cnt[key]:
            deps[key] = self.cnt[key]
        waits = self._waits(e, deps, self_sync=True)
        self.cnt[key] += 16
        ev = (key, self.cnt[key])
        self.ops[e].append((waits, fn, key, 16))
        self._mark(ev, reads, writes)
        return ev

    def _merge_bar(self, deps):
        for k, v in getattr(self, "bar_deps", {}).items():
            if deps.get(k, 0) < v:
                deps[k] = v

    def barrier(self, final=False):
        deps = {k: v for k, v in self.cnt.items() if v > 0}
        self.bar_deps = dict(deps)
        for e in self.ENGS:
            if e == "pool" and not final:
                continue
            waits = self._waits(e, dict(deps), self_sync=True)
            if waits:
                self.ops[e].append((waits, None, None, 0))

    def emit(self):
        sem = self.sem
        with self.nc.Block() as block:
            def mk(e):
                def body(eng):
                    for waits, fn, k, inc in self.ops[e]:
                        for (wk, wv) in waits:
                            eng.wait_ge(sem[wk], wv)
                        if fn is not None:
                            ins = fn(eng)
                            ins.then_inc(sem[k], inc)
                return body
            block.tensor(mk("pe"))
            block.scalar(mk("act"))
            block.vector(mk("dve"))
            block.gpsimd(mk("pool"))
            block.sync(mk("sp"))


def _pieces(n, step=512):
    out = []
    c = 0
    while c < n:
        out.append((c, min(step, n - c)))
        c += step
    return out


def build(NPT, groups, has_s=True, stop_after=None):
    assert sum(groups) == NPT
    nc = bass.Bass("TRN2", target_bir_lowering=False)

    def din(name, shape):
        return nc.dram_tensor(name, list(shape), F32, kind="ExternalInput").ap()

    def dout(name, shape):
        return nc.dram_tensor(name, list(shape), F32, kind="ExternalOutput").ap()

    xp = din("xp", [NPT * 128, D])
    xs = din("xs", [64, D])
    sconv = din("sconv", [48, 3072])
    sssm = din("sssm", [16, 8, 128, 128])
    w_up = [din("w_up1", [2 * NJ, 128, 8, 128]), din("w_up2", [2 * NJ, 128, 8, 128])]
    w_dn = [din("w_dn1", [DFF, D]), din("w_dn2", [DFF, D])]
    w_in = din("w_in", [56, 128, 8, 128])
    w_v = din("w_v", [128, 8, D])
    w_bd = din("w_bd", [128, 8, 16])
    w_ba = din("w_ba", [8, 128, 8, 128])
    w_bb = din("w_bb", [8, 128, 8, 128])
    w_o = din("w_o", [128, 8, D])
    lnbc = din("lnbc", [9, 128, D])
    small_d = din("small", [128, NSM])
    masks_d = din("masks", [128, NMK])
    wsT_d = din("wsT", [128, 8, 128])
    wsTs_d = din("wsTs", [64, 8, 64])
    bsr_d = din("bsr", [1, 8, 128])
    bsrs_d = din("bsrs", [1, 8, 64])

    yp = dout("yp", [NPT * 128, D])
    ys = dout("ys", [64, D])
    ncp = dout("ncp", [3, 3072])
    nsp = dout("nsp", [8, 128, 128])
    ncs = dout("ncs", [48, 3072])
    nss = dout("nss", [16, 8, 128, 128])
    ngv = dout("ngv", [64, D])

    GTP = max(groups)
    GT = GTP + (1 if has_s else 0)
    TPG = GTP * 128
    TG = TPG + (64 if has_s else 0)
    NR = 8

    with ExitStack() as st:
        S = Sched(nc, st)

        uid = [0]

        def sb(name, shape, dt, stack=st):
            uid[0] += 1
            return stack.enter_context(nc.sbuf_tensor("%s_%d" % (name, uid[0]), list(shape), dt))

        def ps(name, shape, dt):
            return st.enter_context(nc.psum_tensor(name, list(shape), dt))

        def A(fn, r=(), w=()):
            return S.op("act", fn, r, w)

        def V(fn, r=(), w=()):
            return S.op("dve", fn, r, w)

        def P(fn, r=(), w=()):
            return S.op("pe", fn, r, w)

        def G(fn, r=(), w=()):
            return S.op("pool", fn, r, w)

        R = sb("R", [128, GT, D], F32)
        bR = [Buf("R%d" % i) for i in range(GT)]
        XT = sb("XT", [128, 8, TG], BF16)
        bXT = [Buf("XT%d" % i) for i in range(GT)]
        ring = sb("ring", [128, NR, 8, 128], BF16)
        bring = [Buf("ring%d" % i) for i in range(NR)]
        gbb = sb("gbb", [128, 2, D], F32)
        bgbb = [Buf("gbb0"), Buf("gbb1")]
        MT = sb("MT", [128, 8, TG], F32)
        bMT = Buf("MT")
        small = sb("small", [128, NSM], F32)
        bsmall = Buf("small")
        masks = sb("masks", [128, NMK], F32)
        bmasks = Buf("masks")
        identb = sb("identb", [128, 128], BF16)
        onesf = sb("onesf", [128, 128], F32)
        onesb = sb("onesb", [1, 128], BF16)
        nbfm = sb("nbfm", [128, 56], F32)
        nea = sb("nea", [128, 8], F32)
        bconst = Buf("const")
        wsTm = sb("wsTm", [128, 8, 128], BF16)
        wsTsm = sb("wsTsm", [64, 8, 64], BF16)
        bsb = sb("bsb", [1, 8, 128], BF16)
        bssb = sb("bssb", [1, 8, 64], BF16)
        Wbd = sb("Wbd", [128, 8, 16], BF16)
        bWbd = Buf("Wbd")
        CARRY = sb("CARRY", [128, 24, 3], F32)
        bCARRY = Buf("CARRY")
        Sst = sb("Sst", [128, 8, 128], F32)
        Sbf = sb("Sbf", [128, 8, 128], BF16)
        bSst = Buf("Sst")
        bSbf = Buf("Sbf")
        mv = sb("mv", [128, GT, 4], F32)
        bmv = Buf("mv")
        st6 = sb("st6", [128, GT, 2, 6], F32)
        bst6 = Buf("st6")
        xb16 = sb("xb16", [128, D], BF16)
        bxb16 = Buf("xb16")

        PB = [ps("PB%d" % i, [128, 512], F32) for i in range(4)]
        bPB = [Buf("PB%d" % i, True) for i in range(4)]
        PS5 = ps("PS5", [128, 512], F32)
        bPS5 = Buf("PS5", True)
        PS6 = ps("PS6", [128, 512], F32)
        bPS6 = Buf("PS6", True)
        PT = ps("PT", [128, 8, 128], BF16)
        bPT = Buf("PT", True)
        PS7 = ps("PS7", [128, 512], F32)
        bPS7 = Buf("PS7", True)

        state = {"rs": 0, "pb": 0}

        def next_bank():
            i = state["pb"] % 4
            state["pb"] += 1
            return PB[i], bPB[i]

        def stream_block(src):
            i = state["rs"] % NR
            state["rs"] += 1
            S.dma("pool", lambda e, i=i, src=src: e.dma_start(
                out=ring[:, i, :, :], in_=src),
                writes=[bring[i]], ring="w", nring=NR)
            return ring[:, i, :, :], bring[i]

        def id_f(n):
            return masks[0:n, MK_ID:MK_ID + n]

        S.dma("sp", lambda e: e.dma_start(out=small[:], in_=small_d), writes=[bsmall])
        S.dma("sp", lambda e: e.dma_start(out=masks[:], in_=masks_d), writes=[bmasks])
        with ExitStack() as cst:
            wsTf = sb("wsTf", [128, 8, 128], F32, cst)
            wsTsf = sb("wsTsf", [64, 8, 64], F32, cst)
            bsf = sb("bsf", [1, 8, 128], F32, cst)
            bssf = sb("bssf", [1, 8, 64], F32, cst)
            btmp = Buf("ctmp")
            S.dma("sp", lambda e: e.dma_start(out=wsTf[:], in_=wsT_d), writes=[btmp])
            S.dma("sp", lambda e: e.dma_start(out=wsTsf[:], in_=wsTs_d), writes=[btmp])
            S.dma("sp", lambda e: e.dma_start(out=bsf[:], in_=bsr_d), writes=[btmp])
            S.dma("sp", lambda e: e.dma_start(out=bssf[:], in_=bsrs_d), writes=[btmp])
            V(lambda e: e.tensor_copy(out=identb[:], in_=masks[:, MK_ID:MK_ID + 128]), [bmasks], [bconst])
            V(lambda e: e.memset(onesf[:], 1.0), [], [bconst])
            V(lambda e: e.memset(onesb[:], 1.0), [], [bconst])
            V(lambda e: e.tensor_scalar_mul(out=nbfm[:], in0=small[:, SM_BFM:SM_BFM + 56], scalar1=-1.0),
              [bsmall], [bconst])
            A(lambda e: e.activation(out=nea[:], in_=small[:, SM_ALOG:SM_ALOG + 8], func=AF.Exp), [bsmall], [bconst])
            V(lambda e: e.tensor_scalar_mul(out=nea[:], in0=nea[:], scalar1=-1.0), [bconst], [bconst])
            V(lambda e: e.tensor_tensor(
                out=wsTm[:], in0=wsTf[:],
                in1=masks[:, MK_GMP:MK_GMP + 128].unsqueeze(1).to_broadcast([128, 8, 128]), op=ALU.mult),
              [btmp, bmasks], [bconst])
            V(lambda e: e.tensor_tensor(
                out=wsTsm[:], in0=wsTsf[:],
                in1=masks[0:64, MK_GMS:MK_GMS + 64].unsqueeze(1).to_broadcast([64, 8, 64]), op=ALU.mult),
              [btmp, bmasks], [bconst])
            V(lambda e: e.tensor_copy(out=bsb[:], in_=bsf[:]), [btmp], [bconst])
            V(lambda e: e.tensor_copy(out=bssb[:], in_=bssf[:]), [btmp], [bconst])
            V(lambda e: e.memset(CARRY[:], 0.0), [], [bCARRY])
            V(lambda e: e.memset(Sst[:], 0.0), [], [bSst])
            V(lambda e: e.memset(Sbf[:], 0.0), [], [bSbf])
            V(lambda e: e.memset(mv[:], 1.0), [], [bmv])
            S.dma("pool", lambda e: e.dma_start(
                out=Wbd[:], in_=w_bd),
                writes=[bWbd], ring="wr", nring=8)
            S.barrier()

        def to_xt(li, col0, pt):
            A(lambda e: e.activation(out=xb16[0:pt, :], in_=R[0:pt, li, :], func=AF.Copy), [bR[li]], [bxb16])

            def tr(e):
                for k in range(8):
                    ins = e.transpose(out=PT[:, k, 0:pt], in_=xb16[0:pt, k * 128:(k + 1) * 128],
                                      identity=identb[0:pt, 0:pt])
                return ins
            P(tr, [bxb16, bconst], [bPT])
            V(lambda e: e.tensor_copy(out=XT[:, :, col0:col0 + pt], in_=PT[:, :, 0:pt]), [bPT], [bXT[li]])

        def load_gb(gi, bi):
            S.dma("sp", lambda e: e.dma_start(out=gbb[:, 0, :], in_=lnbc[gi]), writes=[bgbb[0]])
            S.dma("sp", lambda e: e.dma_start(out=gbb[:, 1, :], in_=lnbc[bi]), writes=[bgbb[1]])

        def layer_norm(tl, src_of, dst_of, bsrc, bdst):
            for (li, col0, pt) in tl:
                def bn(e, li=li, pt=pt):
                    e.bn_stats(out=st6[0:pt, li, 0, :], in_=src_of(li, pt)[:, 0:512])
                    return e.bn_stats(out=st6[0:pt, li, 1, :], in_=src_of(li, pt)[:, 512:1024])
                V(bn, [bsrc[li]], [bst6])
                V(lambda e, li=li, pt=pt: e.bn_aggr(out=mv[0:pt, li, 0:2], in_=st6[0:pt, li, :, :]), [bst6], [bmv])
            lo = min(t[0] for t in tl)
            hi = max(t[0] for t in tl) + 1
            A(lambda e: e.activation(out=mv[:, lo:hi, 2], in_=mv[:, lo:hi, 1], func=AF.Ln, bias=LN_EPS), [bmv], [bmv])
            A(lambda e: e.activation(out=mv[:, lo:hi, 3], in_=mv[:, lo:hi, 2], func=AF.Exp, scale=-0.5), [bmv], [bmv])
            for (li, col0, pt) in tl:
                V(lambda e, li=li, pt=pt: e.tensor_scalar(
                    out=dst_of(li, pt), in0=src_of(li, pt), scalar1=mv[0:pt, li, 0:1], scalar2=mv[0:pt, li, 3:4],
                    op0=ALU.subtract, op1=ALU.mult), [bmv, bsrc[li]], [bdst[li]])
                V(lambda e, li=li, pt=pt: e.tensor_tensor(
                    out=dst_of(li, pt), in0=dst_of(li, pt), in1=gbb[0:pt, 0, :], op=ALU.mult),
                  [bgbb[0], bdst[li]], [bdst[li]])
                V(lambda e, li=li, pt=pt: e.tensor_tensor(
                    out=dst_of(li, pt), in0=dst_of(li, pt), in1=gbb[0:pt, 1, :], op=ALU.add),
                  [bgbb[1], bdst[li]], [bdst[li]])

        def gemm_fm(blocks, rhsT, rbufs_of_piece, pieces, epi, after_block=None):
            for bi, src in enumerate(blocks):
                if after_block is not None and bi > 0:
                    after_block(bi - 1)
                slot, bslot = stream_block(src)
                for (c0, n) in pieces:
                    bank, bbank = next_bank()

                    def mm(e, slot=slot, bank=bank, c0=c0, n=n):
                        for k in range(8):
                            ins = e.matmul(bank[:, 0:n], lhsT=slot[:, k, :], rhs=rhsT[:, k, c0:c0 + n],
                                           start=(k == 0), stop=(k == 7))
                        return ins
                    P(mm, [bslot] + rbufs_of_piece(c0, n), [bbank])
                    epi(bi, c0, n, bank, bbank)

        ngroups = len(groups)

        def do_group(gidx, gsz, tile_base):
            last = (gidx == ngroups - 1)
            sam = has_s and last
            Tpg = gsz * 128
            Tg = Tpg + (64 if sam else 0)
            tl = [(li, li * 128, 128) for li in range(gsz)]
            if sam:
                tl.append((gsz, Tpg, 64))
            ppieces = _pieces(Tpg)
            pieces = list(ppieces) + ([(Tpg, 64)] if sam else [])

            def xt_bufs(c0, n, tl=tl):
                return [bXT[li] for (li, col0, pt) in tl if col0 < c0 + n and col0 + pt > c0]

            for (li, col0, pt) in tl:
                if pt == 128:
                    r0 = (tile_base + li) * 128
                    S.dma("sp", lambda e, li=li, r0=r0: e.dma_start(out=R[:, li, :], in_=xp[r0:r0 + 128, :]),
                          writes=[bR[li]])
                else:
                    S.dma("sp", lambda e, li=li: e.dma_start(out=R[0:64, li, :], in_=xs), writes=[bR[li]])
                to_xt(li, col0, pt)

            def ffn(which, final):
                wu, wd = w_up[which], w_dn[which]
                load_gb(0 if which == 0 else 4, 1 if which == 0 else 5)
                with ExitStack() as sc:
                    HT = sb("HT", [128, NJ, TG], BF16, sc)
                    bHT = [Buf("HT%d" % j) for j in range(NJ)]
                    Wdn = sb("Wdn", [128, NJ, D], BF16, sc)
                    bWdn = [Buf("Wdn%d" % j) for j in range(NJ)]
                    tA = [sb("tA%d" % i, [128, 512], F32, sc) for i in range(2)]
                    btA = [Buf("tA0"), Buf("tA1")]
                    tB = [sb("tB%d" % i, [128, 512], F32, sc) for i in range(2)]
                    btB = [Buf("tB0"), Buf("tB1")]
                    cnt = 0
                    for j in range(NJ):
                        sa, bsa = stream_block(wu[j])
                        sg, bsg = stream_block(wu[NJ + j])
                        S.dma("pool", lambda e, j=j: e.dma_start(out=Wdn[:, j, :], in_=wd[j * 128:(j + 1) * 128, :]),
                              writes=[bWdn[j]], ring="wr", nring=8)
                        for (c0, n) in pieces:
                            ba, bba = next_bank()
                            bg, bbg = next_bank()

                            def mm(e, slot=sa, bank=ba, c0=c0, n=n):
                                for k in range(8):
                                    ins = e.matmul(bank[:, 0:n], lhsT=slot[:, k, :], rhs=XT[:, k, c0:c0 + n],
                                                   start=(k == 0), stop=(k == 7))
                                return ins
                            P(mm, [bsa] + xt_bufs(c0, n), [bba])

                            def mm2(e, slot=sg, bank=bg, c0=c0, n=n):
                                for k in range(8):
                                    ins = e.matmul(bank[:, 0:n], lhsT=slot[:, k, :], rhs=XT[:, k, c0:c0 + n],
                                                   start=(k == 0), stop=(k == 7))
                                return ins
                            P(mm2, [bsg] + xt_bufs(c0, n), [bbg])
                            ta, bta = tA[cnt % 2], btA[cnt % 2]
                            tb, btb = tB[cnt % 2], btB[cnt % 2]
                            cnt += 1
                            A(lambda e, ta=ta, ba=ba, n=n: e.activation(out=ta[:, 0:n], in_=ba[:, 0:n], func=AF.Silu),
                              [bba], [bta])
                            V(lambda e, ta=ta, bg=bg, j=j, c0=c0, n=n: e.scalar_tensor_tensor(
                                out=HT[:, j, c0:c0 + n], in0=ta[:, 0:n], scalar=0.5, in1=bg[:, 0:n],
                                op0=ALU.mult, op1=ALU.mult), [bta, bbg], [bHT[j]])
                    def down_tile(li, col0, pt):
                        for half in range(2):
                            bank, bbank = next_bank()

                            def mmd(e, bank=bank, col0=col0, pt=pt, half=half):
                                for j in range(NJ):
                                    ins = e.matmul(bank[0:pt, :], lhsT=HT[:, j, col0:col0 + pt],
                                                   rhs=Wdn[:, j, half * 512:(half + 1) * 512],
                                                   start=(j == 0), stop=(j == NJ - 1))
                                return ins
                            P(mmd, bHT + bWdn, [bbank])
                            V(lambda e, bank=bank, li=li, pt=pt, half=half: e.scalar_tensor_tensor(
                                out=R[0:pt, li, half * 512:(half + 1) * 512],
                                in0=R[0:pt, li, half * 512:(half + 1) * 512], scalar=ALPHA, in1=bank[0:pt, :],
                                op0=ALU.mult, op1=ALU.add), [bbank, bR[li]], [bR[li]])

                    def finish_tile(li, col0, pt):
                        layer_norm([(li, col0, pt)], lambda li_, pt_: R[0:pt_, li_, :],
                                   lambda li_, pt_: R[0:pt_, li_, :], bR, bR)
                        if final:
                            if pt == 128:
                                r0 = (tile_base + li) * 128
                                S.dma("sp", lambda e, li=li, r0=r0: e.dma_start(out=yp[r0:r0 + 128, :],
                                                                               in_=R[:, li, :]),
                                      reads=[bR[li]], writes=[Buf()])
                            else:
                                S.dma("sp", lambda e, li=li: e.dma_start(out=ys, in_=R[0:64, li, :]),
                                      reads=[bR[li]], writes=[Buf()])
                        else:
                            to_xt(li, col0, pt)
                    prev = None
                    for t_ in tl:
                        down_tile(*t_)
                        if prev is not None:
                            finish_tile(*prev)
                        prev = t_
                    finish_tile(*prev)
                    S.barrier()

            ffn(0, final=(stop_after == "ffn1"))
            if stop_after == "ffn1":
                return

            with ExitStack() as sB:
                ZT = sb("ZT", [128, 8, TG], BF16, sB)
                bZT = Buf("ZT")
                with ExitStack() as sc:
                    UT = sb("UT", [128, 8, TG], BF16, sc)
                    bUT = Buf("UT")
                    WsmA = sb("Wsm", [128, 8, D], BF16, sc)
                    bWsmA = Buf("Wsm")
                    t0A = sb("t0", [128, 512], F32, sc)
                    t1A = sb("t1", [128, 512], F32, sc)
                    bt0A, bt1A = Buf("t0"), Buf("t1")
                    vx = sb("vx", [128, D], F32, sc)
                    vy = sb("vy", [128, D], F32, sc)
                    bvx, bvy = [Buf("vx")], [Buf("vy")]
                    VnA = sb("VnA", [128, GT, D], BF16, sc)
                    bVnA = [Buf("VnA%d" % i) for i in range(GT)]
                    bvb = sb("bvb", [128, D], F32, sc)
                    bbvb = Buf("bvb")
                    S.dma("pool", lambda e: e.dma_start(
                        out=WsmA[:], in_=w_v),
                        writes=[bWsmA], ring="wr", nring=8)
                    S.dma("sp", lambda e: e.dma_start(out=bvb[:], in_=lnbc[8]), writes=[bbvb])
                    load_gb(6, 7)

                    def gelu_chain(x, sq, n, out_ap, rb, wb, bx_, bsq_):
                        V(lambda e: e.scalar_tensor_tensor(out=sq, in0=sq, scalar=0.044715, in1=x,
                                                           op0=ALU.mult, op1=ALU.mult), [bx_, bsq_], [bsq_])
                        V(lambda e: e.tensor_tensor(out=sq, in0=sq, in1=x, op=ALU.add), [bx_, bsq_], [bsq_])
                        A(lambda e: e.activation(out=sq, in_=sq, func=AF.Exp, scale=-2.0 * GELU_K), [bsq_], [bsq_])
                        V(lambda e: e.tensor_scalar_add(out=sq, in0=sq, scalar1=1.0), [bsq_], [bsq_])
                        V(lambda e: e.reciprocal(out=sq, in_=sq), [bsq_], [bsq_])
                        V(lambda e: e.tensor_tensor(out=out_ap, in0=sq, in1=x, op=ALU.mult), [bx_, bsq_] + rb, wb)

                    def epi_u(bi, c0, n, bank, bbank):
                        A(lambda e: e.activation(out=UT[:, bi, c0:c0 + n], in_=bank[:, 0:n], func=AF.Gelu_apprx_tanh,
                                                 bias=small[:, SM_BFM + bi:SM_BFM + bi + 1]),
                          [bbank, bsmall], [bUT])
                    def v_front(li, col0, pt):
                        for half in range(2):
                            bank, bbank = next_bank()

                            def mmv(e, bank=bank, col0=col0, pt=pt, half=half):
                                for k in range(8):
                                    ins = e.matmul(bank[0:pt, :], lhsT=XT[:, k, col0:col0 + pt],
                                                   rhs=WsmA[:, k, half * 512:(half + 1) * 512],
                                                   start=(k == 0), stop=(k == 7))
                                return ins
                            P(mmv, [bXT[li], bWsmA], [bbank])
                            V(lambda e, bank=bank, pt=pt, half=half: e.tensor_tensor(
                                out=vx[0:pt, half * 512:(half + 1) * 512], in0=bank[0:pt, :],
                                in1=bvb[0:pt, half * 512:(half + 1) * 512], op=ALU.add), [bbank, bbvb], bvx)
                        A(lambda e, pt=pt: e.activation(out=vx[0:pt, :], in_=vx[0:pt, :], func=AF.Gelu_apprx_tanh),
                          bvx, bvx)
                        layer_norm([(0, 0, pt)], lambda li_, pt_: vx[0:pt_, :], lambda li_, pt_: vy[0:pt_, :],
                                   bvx, bvy)
                        if pt == 64:
                            S.dma("sp", lambda e: e.dma_start(out=ngv, in_=vy[0:64, :]), reads=bvy, writes=[Buf()])
                        A(lambda e, pt=pt, li=li: e.activation(out=VnA[0:pt, li, :], in_=vy[0:pt, :], func=AF.Copy),
                          bvy, [bVnA[li]])

                    def v_back(li, col0, pt):
                        wm = wsTm if pt == 128 else wsTsm
                        bs_ = bsb if pt == 128 else bssb
                        for g0 in (0, 4):
                            bank, bbank = next_bank()

                            def mmg(e, bank=bank, pt=pt, g0=g0, wm=wm, bs_=bs_, li=li):
                                for g in range(g0, g0 + 4):
                                    o_ = bank[:, (g - g0) * 128:(g - g0) * 128 + pt]
                                    e.matmul(o_, lhsT=VnA[0:pt, li, g * 128:(g + 1) * 128], rhs=wm[0:pt, g, 0:pt],
                                             start=True, stop=False)
                                    ins = e.matmul(o_, lhsT=onesb[0:1, 0:128], rhs=bs_[0:1, g, 0:pt],
                                                   start=False, stop=True)
                                return ins
                            P(mmg, [bVnA[li], bconst], [bbank])
                            V(lambda e, bank=bank, pt=pt, g0=g0, col0=col0: e.tensor_tensor(
                                out=UT[:, g0:g0 + 4, col0:col0 + pt], in0=UT[:, g0:g0 + 4, col0:col0 + pt],
                                in1=bank[:, :].rearrange("p (g t) -> p g t", g=4)[:, :, 0:pt], op=ALU.mult),
                              [bbank, bUT], [bUT])

                    vq = list(tl)

                    def after_u(bi):
                        if vq:
                            v_front(*vq.pop(0))
                    gemm_fm([w_in[c] for c in range(8)], XT, xt_bufs,
                            pieces, epi_u, after_block=after_u)
                    while vq:
                        v_front(*vq.pop(0))
                    for t_ in tl:
                        v_back(*t_)

                    def epi_ga(bi, c0, n, bank, bbank):
                        A(lambda e: e.activation(out=MT[:, bi, c0:c0 + n], in_=bank[:, 0:n], func=AF.Sigmoid,
                                                 bias=small[:, SM_BFM + 40 + bi:SM_BFM + 41 + bi]),
                          [bbank, bsmall], [bMT])
                    gemm_fm([w_in[40 + c] for c in range(8)], XT, xt_bufs,
                            pieces, epi_ga)

                    def epi_ba(bi, c0, n, bank, bbank):
                        V(lambda e: e.tensor_tensor(out=MT[:, bi, c0:c0 + n], in0=MT[:, bi, c0:c0 + n],
                                                    in1=bank[:, 0:n], op=ALU.mult), [bbank, bMT], [bMT])
                    gemm_fm([w_ba[c] for c in range(8)], UT, lambda c0, n: [bUT],
                            pieces, epi_ba)
                    S.barrier()
                if stop_after == "ba":
                    return

                with ExitStack() as sQ:
                    QT = sb("QT", [128, 8, TG], BF16, sQ)
                    KT = sb("KT", [128, 8, TG], BF16, sQ)
                    VT = sb("VT", [128, 8, TG], BF16, sQ)
                    bQT, bKT, bVT = Buf("QT"), Buf("KT"), Buf("VT")
                    NCS = sb("NCS", [128, 24, 48], F32, sQ)
                    bNCS = Buf("NCS")
                    with ExitStack() as sc:
                        PREs = [sb("PRE", [128, 3 + TPG], F32, sc) for _ in range(4)]
                        bPREs = [Buf("PRE%d" % i) for i in range(4)]
                        PRESs = [sb("PRES", [128, 16, 7], F32, sc) for _ in range(4)]
                        bPRESs = [Buf("PRES%d" % i) for i in range(4)]
                        accs_ = [sb("acc", [128, TG], F32, sc) for _ in range(4)]
                        baccs = [Buf("acc%d" % i) for i in range(4)]
                        exs = [sb("ex", [128, TG], F32, sc) for _ in range(4)]
                        bexs = [Buf("ex%d" % i) for i in range(4)]
                        t0Bs = [sb("t0b", [128, 512], F32, sc) for _ in range(2)]
                        bt0Bs = [Buf("t0b0"), Buf("t0b1")]
                        rrs = [sb("rr", [128, TG], F32, sc) for _ in range(4)]
                        brrs = [Buf("rr%d" % i) for i in range(4)]
                        if sam:
                            SCin = sb("SCin", [48, 3072], F32, sc)
                            bSCin = Buf("SCin")
                            SCT = sb("SCT", [128, 24, 48], F32, sc)
                            bSCT = Buf("SCT")
                            S.dma("sp", lambda e: e.dma_start(out=SCin[:], in_=sconv), writes=[bSCin])
                            for c in range(24):
                                P(lambda e, c=c: e.transpose(out=PS5[:, 0:48], in_=SCin[0:48, c * 128:(c + 1) * 128],
                                                             identity=id_f(48)), [bSCin, bmasks], [bPS5])
                                V(lambda e, c=c: e.tensor_copy(out=SCT[:, c, :], in_=PS5[:, 0:48]), [bPS5], [bSCT])
                        def qkv_block(c):
                            PRE, bPRE = PREs[c % 4], bPREs[c % 4]
                            PRES, bPRES = PRESs[c % 4], bPRESs[c % 4]
                            acc, bacc = accs_[c % 4], baccs[c % 4]
                            ex, bex = exs[c % 4], bexs[c % 4]
                            t0B, bt0B = t0Bs[c % 2], bt0Bs[c % 2]
                            slot, bslot = stream_block(w_in[8 + c])
                            bcol = small[:, SM_BFM + 8 + c:SM_BFM + 9 + c]
                            cw = [small[:, SM_CONV + c * 4 + i:SM_CONV + c * 4 + i + 1] for i in range(4)]
                            V(lambda e, c=c: e.tensor_copy(out=PRE[:, 0:3], in_=CARRY[:, c, :]), [bCARRY], [bPRE])
                            for (c0, n) in pieces:
                                bank, bbank = next_bank()

                                def mm(e, slot=slot, bank=bank, c0=c0, n=n):
                                    for k in range(8):
                                        ins = e.matmul(bank[:, 0:n], lhsT=slot[:, k, :], rhs=XT[:, k, c0:c0 + n],
                                                       start=(k == 0), stop=(k == 7))
                                    return ins
                                P(mm, [bslot] + xt_bufs(c0, n), [bbank])
                                if c0 < Tpg:
                                    A(lambda e, bank=bank, c0=c0, n=n, bcol=bcol: e.activation(
                                        out=PRE[:, 3 + c0:3 + c0 + n], in_=bank[:, 0:n], func=AF.Identity, bias=bcol),
                                      [bbank, bsmall], [bPRE])
                                else:
                                    A(lambda e, bank=bank, bcol=bcol: e.activation(
                                        out=PRES[:, :, 3:7], in_=bank[:, 0:64].rearrange("p (s t) -> p s t", t=4),
                                        func=AF.Identity, bias=bcol), [bbank, bsmall], [bPRES])
                            V(lambda e, c=c: e.tensor_copy(out=CARRY[:, c, :], in_=PRE[:, Tpg:Tpg + 3]),
                              [bPRE], [bCARRY])
                            yield
                            V(lambda e, cw=cw: e.tensor_scalar(out=acc[:, 0:Tpg], in0=PRE[:, 0:Tpg], scalar1=cw[0],
                                                               scalar2=None, op0=ALU.mult), [bPRE, bsmall], [bacc])
                            for i in range(1, 4):
                                V(lambda e, cw=cw, i=i: e.scalar_tensor_tensor(
                                    out=acc[:, 0:Tpg], in0=PRE[:, i:i + Tpg], scalar=cw[i], in1=acc[:, 0:Tpg],
                                    op0=ALU.mult, op1=ALU.add), [bPRE, bsmall, bacc], [bacc])
                            if sam:
                                V(lambda e, c=c: e.tensor_copy(
                                    out=PRES[:, :, 0:3], in_=SCT[:, c, :].rearrange("p (s r) -> p s r", r=3)),
                                  [bSCT], [bPRES])
                                V(lambda e, c=c: e.tensor_copy(
                                    out=NCS[:, c, :].rearrange("p (s r) -> p s r", r=3), in_=PRES[:, :, 4:7]),
                                  [bPRES], [bNCS])
                                accs = acc[:, Tpg:Tpg + 64].rearrange("p (s t) -> p s t", t=4)
                                V(lambda e, cw=cw, accs=accs: e.tensor_scalar(
                                    out=accs, in0=PRES[:, :, 0:4], scalar1=cw[0], scalar2=None, op0=ALU.mult),
                                  [bPRES, bsmall], [bacc])
                                for i in range(1, 4):
                                    V(lambda e, cw=cw, i=i, accs=accs: e.scalar_tensor_tensor(
                                        out=accs, in0=PRES[:, :, i:i + 4], scalar=cw[i], in1=accs,
                                        op0=ALU.mult, op1=ALU.add), [bPRES, bsmall, bacc], [bacc])
                            h = c % 8
                            if c >= 16:
                                A(lambda e, h=h: e.activation(out=VT[:, h, 0:Tg], in_=acc[:, 0:Tg], func=AF.Silu),
                                  [bacc], [bVT])
                                yield
                            else:
                                A(lambda e: e.activation(out=ex[:, 0:Tg], in_=acc[:, 0:Tg], func=AF.Silu),
                                  [bacc], [bex])
                                A(lambda e: e.activation(out=acc[:, 0:Tg], in_=ex[:, 0:Tg], func=AF.Square),
                                  [bex], [bacc])
                                yield
                                dst, bdst = (QT, bQT) if c < 8 else (KT, bKT)
                                qb = (-0.5 * math.log(128.0)) if c < 8 else 0.0
                                rr, brr = rrs[c % 4], brrs[c % 4]
                                for pi_, (c0, n) in enumerate(_pieces(Tg)):
                                    pso, bpso = [(PS6, bPS6), (PS5, bPS5), (PS7, bPS7)][(2 * (c % 2) + pi_) % 3]
                                    P(lambda e, c0=c0, n=n, pso=pso: e.matmul(pso[:, 0:n], lhsT=onesf[:],
                                                                              rhs=acc[:, c0:c0 + n],
                                                                              start=True, stop=True),
                                      [bacc, bconst], [bpso])
                                    A(lambda e, c0=c0, n=n, pso=pso: e.activation(out=rr[:, c0:c0 + n], in_=pso[:, 0:n],
                                                                                  func=AF.Ln, bias=RMS_EPS),
                                      [bpso], [brr])
                                    A(lambda e, c0=c0, n=n, qb=qb: e.activation(out=rr[:, c0:c0 + n],
                                                                                in_=rr[:, c0:c0 + n], func=AF.Exp,
                                                                                scale=-0.5, bias=qb), [brr], [brr])
                                yield
                                V(lambda e, dst=dst, h=h: e.tensor_tensor(
                                    out=dst[:, h, 0:Tg], in0=ex[:, 0:Tg], in1=rr[:, 0:Tg], op=ALU.mult),
                                  [brr, bex], [bdst])

                        qgens = [qkv_block(c) for c in range(24)]

                        def qstep(c):
                            next(qgens[c])

                        def qfin(c):
                            for _ in qgens[c]:
                                pass
                        for k in range(0, 15):
                            if k < 12:
                                qstep(2 * k)
                                qstep(2 * k + 1)
                            if 3 <= k <= 14:
                                qfin(2 * k - 6)
                                qfin(2 * k - 5)
                            if 1 <= k <= 12:
                                qstep(2 * k - 2)
                                qstep(2 * k - 1)
                            if 2 <= k <= 13:
                                for c_ in (2 * k - 4, 2 * k - 3):
                                    if c_ < 16:
                                        qstep(c_)

                        def epi_z(bi, c0, n, bank, bbank):
                            A(lambda e: e.activation(out=ZT[:, bi, c0:c0 + n], in_=bank[:, 0:n], func=AF.Silu,
                                                     bias=small[:, SM_BFM + 32 + bi:SM_BFM + 33 + bi]),
                              [bbank, bsmall], [bZT])
                        gemm_fm([w_in[32 + c] for c in range(8)], XT, xt_bufs,
                                pieces, epi_z)

                        if last:
                            stg = SCin if sam else sb("stg", [48, 3072], F32, sc)
                            bstg = Buf("stg")
                            S.barrier()
                            for c in range(24):
                                P(lambda e, c=c: e.transpose(out=PS5[0:3, 0:128], in_=CARRY[:, c, :],
                                                             identity=id_f(128)), [bCARRY, bmasks], [bPS5])
                                V(lambda e, c=c: e.tensor_copy(out=stg[0:3, c * 128:(c + 1) * 128],
                                                               in_=PS5[0:3, 0:128]), [bPS5], [bstg])
                            S.dma("sp", lambda e: e.dma_start(out=ncp, in_=stg[0:3, :]), reads=[bstg], writes=[Buf()])
                            if sam:
                                S.barrier()
                                for c in range(24):
                                    P(lambda e, c=c: e.transpose(out=PS5[0:48, 0:128], in_=NCS[:, c, :],
                                                                 identity=id_f(128)), [bNCS, bmasks], [bPS5])
                                    V(lambda e, c=c: e.tensor_copy(out=stg[0:48, c * 128:(c + 1) * 128],
                                                                   in_=PS5[0:48, 0:128]), [bPS5], [bstg])
                                S.dma("sp", lambda e: e.dma_start(out=ncs, in_=stg[0:48, :]), reads=[bstg],
                                      writes=[Buf()])
                        S.barrier()

                    if stop_after == "bb1":
                        return
                    with ExitStack() as sc:
                        def f32t(name):
                            return sb(name, [128, 8, 64], F32, sc)

                        def b16t(name, w=64):
                            return sb(name, [128, 8, w], BF16, sc)
                        units = []
                        if sam:
                            units.append((Tpg, True, 1, gsz))
                        for t_ in range(gsz):
                            units.append((t_ * 128, False, 2, t_))
                        NT_ = gsz + (1 if sam else 0)
                        gcs = sb("gcs", [128, 8], F32, sc)
                        gb_ = sb("gb_", [128, 8, 128], F32, sc)
                        Dm = f32t("Dm")
                        Ds = f32t("Ds")
                        Xm = f32t("Xm")
                        Tt = f32t("Tt")
                        Xb = b16t("Xb")
                        Wb = b16t("Wb")
                        Pw1 = b16t("Pw1")
                        Px1 = b16t("Px1")
                        cls = [sb("cl", [128, 8], F32, sc) for _ in range(2)]
                        EGs = [sb("EG", [128, 2, 8, 64], F32, sc) for _ in range(2)]
                        kgs = [sb("kg", [128, 2, 8, 64], BF16, sc) for _ in range(2)]
                        qgs = [sb("qg", [128, 2, 8, 64], BF16, sc) for _ in range(2)]
                        qkms = [b16t("qkm") for _ in range(2)]
                        Ttbs = [b16t("Ttb") for _ in range(2)]
                        kds = [b16t("kd", 128) for _ in range(2)]
                        vtms = [b16t("vtm", 128) for _ in range(2)]
                        smA = sb("smA", [128, 8, NT_, 8], F32, sc)
                        bsmA = Buf("smA")
                        bdA = sb("bdA", [128, NT_, 16], F32, sc)
                        bbdA = Buf("bdA")
                        vd = b16t("vd", 128)
                        vnew = b16t("vnew", 128)
                        osq = sb("osq", [128, 512], F32, sc)
                        orr = sb("orr", [128, 512], F32, sc)
                        ocp = sb("ocp", [128, 512], F32, sc)
                        bb = {n: Buf(n) for n in ("gcs", "gb", "Dm", "Ds", "Xm", "Tt", "Xb", "Wb", "Pw1", "Px1",
                                                  "vd", "vnew", "osq", "orr", "ocp")}
                        b2 = {n: [Buf(n + "0"), Buf(n + "1")] for n in ("cl", "EG", "kg", "qg", "qkm", "Ttb", "kd",
                                                                         "vtm")}
                        if sam:
                            Ssbs = [sb("Ssb", [128, 8, 128], F32, sc) for _ in range(2)]
                            Sb16s = [sb("Sb16", [128, 8, 128], BF16, sc) for _ in range(2)]
                            kdms = [sb("kdm", [64, 8, 128], BF16, sc) for _ in range(2)]
                            vdT = sb("vdT", [128, 8, 64], BF16, sc)
                            oS = sb("oS", [128, 512], F32, sc)
                            for n in ("vdT", "oS"):
                                bb[n] = Buf(n)
                            for n in ("Ssb", "Sb16", "kdm"):
                                b2[n] = [Buf(n + "0"), Buf(n + "1")]
                        pA, bpA = PB[0], bPB[0]
                        pB, bpB = PB[1], bPB[1]
                        pB2, bpB2 = PB[2], bPB[2]
                        pC, bpC = PB[3], bPB[3]
                        c1, bc1 = PS5, bPS5
                        c2, bc2 = PS6, bPS6
                        c3, bc3 = PS7, bPS7

                        def v3(ap, r0, r1):
                            return ap[r0:r1, :].rearrange("p (h i) -> p h i", h=8)

                        def v4(ap, r0, r1):
                            return ap[r0:r1, :].rearrange("p (h d) -> p h d", h=4)

                        def tp(hf):
                            return (64, 64) if hf else None

                        def tpk(hf):
                            return (0, 64) if hf else None

                        def mmx(e, out, lhsT, rhs, tpos, start=True, stop=True):
                            if tpos is None:
                                return e.matmul(out, lhsT=lhsT, rhs=rhs, start=start, stop=stop)
                            return e.matmul(out, lhsT=lhsT, rhs=rhs, start=start, stop=stop, tile_position=tpos)

                        for (ucol, uis, unh, ut) in units:
                            def mmbd(e, ucol=ucol, unh=unh, ut=ut):
                                m_rows = 64 * unh
                                for k in range(8):
                                    ins = e.matmul(PS5[0:m_rows, ut * 16:(ut + 1) * 16],
                                                   lhsT=XT[:, k, ucol:ucol + m_rows], rhs=Wbd[:, k, :],
                                                   start=(k == 0), stop=(k == 7))
                                return ins
                            P(mmbd, xt_bufs(ucol, 64 * unh) + [bWbd], [bPS5])
                        if sam:
                            V(lambda e: e.memset(bdA[64:128, gsz, :], 0.0), [], [bbdA])
                        for (ucol, uis, unh, ut) in units:
                            V(lambda e, unh=unh, ut=ut: e.tensor_tensor(
                                out=bdA[0:64 * unh, ut, :], in0=PS5[0:64 * unh, ut * 16:(ut + 1) * 16],
                                in1=small[0:64 * unh, SM_BBD:SM_BBD + 16], op=ALU.add), [bPS5, bsmall], [bbdA])
                        r_ = [smA[:, i, :, :] for i in range(8)]
                        e_, beta_, nbeta_, xd_, m_, na_, gl_ = r_[0], r_[1], r_[2], r_[3], r_[4], r_[5], r_[6]
                        A(lambda e: e.activation(out=e_, in_=bdA[:, :, 0:8], func=AF.Exp, scale=-1.0), [bbdA], [bsmA])
                        V(lambda e: e.tensor_tensor(
                            out=xd_, in0=bdA[:, :, 8:16],
                            in1=small[:, SM_DTB:SM_DTB + 8].unsqueeze(1).to_broadcast([128, NT_, 8]), op=ALU.add),
                          [bbdA, bsmall], [bsmA])
                        V(lambda e: e.tensor_scalar_max(out=m_, in0=xd_, scalar1=0.0), [bsmA], [bsmA])
                        V(lambda e: e.scalar_tensor_tensor(out=na_, in0=xd_, scalar=0.0, in1=m_, op0=ALU.min,
                                                           op1=ALU.subtract), [bsmA], [bsmA])
                        A(lambda e: e.activation(out=na_, in_=na_, func=AF.Exp), [bsmA], [bsmA])
                        V(lambda e: e.tensor_scalar_add(out=e_, in0=e_, scalar1=1.0), [bsmA], [bsmA])
                        A(lambda e: e.activation(out=na_, in_=na_, func=AF.Ln, bias=1.0), [bsmA], [bsmA])
                        V(lambda e: e.reciprocal(out=beta_, in_=e_), [bsmA], [bsmA])
                        V(lambda e: e.tensor_scalar_mul(out=nbeta_, in0=beta_, scalar1=-1.0), [bsmA], [bsmA])
                        V(lambda e: e.tensor_tensor(out=gl_, in0=na_, in1=m_, op=ALU.add), [bsmA], [bsmA])
                        V(lambda e: e.tensor_tensor(
                            out=gl_, in0=gl_, in1=nea[:, :].unsqueeze(1).to_broadcast([128, NT_, 8]), op=ALU.mult),
                          [bsmA, bconst], [bsmA])

                        def pre(ucol, is_s, nh, ut, p):
                            R1 = 64 * nh
                            mo = lambda a, b_: masks[0:R1, (a if is_s else b_):(a if is_s else b_) + 64]
                            cau = mo(MK_CAUS, MK_CAUP)
                            stri = mo(MK_STRS, MK_STRP)
                            suf = mo(MK_SUFS, MK_SUFP)
                            id2 = masks[0:R1, MK_ID2:MK_ID2 + 64]
                            EG, kg, qg, qkm, Ttb, kd, vtm = (EGs[p], kgs[p], qgs[p], qkms[p], Ttbs[p],
                                                            kds[p], vtms[p])
                            bEG, bkg, bqg, bqkm, bTtb, bkd, bvtm = (b2["EG"][p], b2["kg"][p],
                                                                    b2["qg"][p], b2["qkm"][p], b2["Ttb"][p],
                                                                    b2["kd"][p], b2["vtm"][p])
                            nbeta, gl = smA[0:R1, 2, ut, :], smA[0:R1, 6, ut, :]
                            cl, bcl = cls[p][0:R1, :], b2["cl"][p]
                            hv = list(range(nh))
                            rs = lambda hf: slice(64 * hf, 64 * hf + 64)
                            cs = lambda hf: slice(ucol + 64 * hf, ucol + 64 * hf + 64)
                            gbank = [(pB, bpB), (pB2, bpB2)]
                            V(lambda e: e.tensor_copy(out=gb_[0:R1], in_=gl.unsqueeze(2).to_broadcast([R1, 8, 128])),
                              [bsmA], [bb["gb"]])
                            yield

                            def mmg(e):
                                for hf in hv:
                                    mmx(e, pA[rs(hf), 16:24], cau[rs(hf), :], smA[rs(hf), 6, ut, :], tp(hf))
                                    ins = mmx(e, pA[rs(hf), 24:32], suf[rs(hf), :], smA[rs(hf), 6, ut, :], tp(hf))
                                return ins
                            P(mmg, [bsmA, bmasks], [bpA])

                            def mmgb(e):
                                for hf in hv:
                                    for h in range(8):
                                        ins = e.matmul(gbank[hf][0][:, h * 64:(h + 1) * 64], lhsT=gb_[rs(hf), h, :],
                                                       rhs=cau[rs(hf), :], start=True, stop=True)
                                return ins
                            P(mmgb, [bb["gb"], bmasks], [bpB, bpB2])
                            yield
                            V(lambda e: e.tensor_copy(out=gcs[0:R1], in_=pA[0:R1, 16:24]), [bpA], [bb["gcs"]])
                            for hf in hv:
                                A(lambda e, hf=hf: e.activation(out=EG[:, hf], in_=v3(gbank[hf][0], 0, 128), func=AF.Exp),
                                  [gbank[hf][1]], [bEG])
                            yield
                            A(lambda e: e.activation(out=cl, in_=pA[0:R1, 24:32], func=AF.Exp), [bpA], [bcl])
                            for hf in hv:
                                V(lambda e, hf=hf: e.tensor_tensor(
                                    out=Dm[rs(hf)], in0=v3(gbank[hf][0], 64 * hf, 64 * hf + 64),
                                    in1=gcs[rs(hf)].unsqueeze(2).to_broadcast([64, 8, 64]),
                                    op=ALU.subtract), [gbank[hf][1], bb["gcs"]], [bb["Dm"]])
                            yield

                            def trk(e):
                                for hf in hv:
                                    for h in range(8):
                                        if hf:
                                            ins = e.transpose(out=PT[rs(hf), h, :], in_=KT[:, h, cs(hf)],
                                                              identity=identb[:], tile_position=(0, 64))
                                        else:
                                            ins = e.transpose(out=PT[rs(hf), h, :], in_=KT[:, h, cs(hf)],
                                                              identity=identb[:])
                                return ins
                            P(trk, [bKT, bconst], [bPT])
                            yield
                            V(lambda e: e.tensor_tensor(out=kd[0:R1], in0=PT[0:R1, :, :],
                                                        in1=cl.unsqueeze(2).to_broadcast([R1, 8, 128]), op=ALU.mult),
                              [bPT, bcl], [bkd])
                            yield
                            if not is_s:
                                def trv(e):
                                    for hf in hv:
                                        for h in range(8):
                                            if hf:
                                                ins = e.transpose(out=PT[rs(hf), h, :], in_=VT[:, h, cs(hf)],
                                                                  identity=identb[:], tile_position=(0, 64))
                                            else:
                                                ins = e.transpose(out=PT[rs(hf), h, :], in_=VT[:, h, cs(hf)],
                                                                  identity=identb[:])
                                    return ins
                                P(trv, [bVT, bconst], [bPT])
                                yield
                                A(lambda e: e.activation(out=vtm[0:R1], in_=PT[0:R1, :, :], func=AF.Copy),
                                  [bPT], [bvtm])
                                yield

                            def mmkk(e):
                                for hf in hv:
                                    for h in range(8):
                                        mmx(e, pA[rs(hf), h * 64:(h + 1) * 64], KT[:, h, cs(hf)], KT[:, h, cs(hf)],
                                            tpk(hf))
                                        ins = mmx(e, pB[rs(hf), h * 64:(h + 1) * 64], KT[:, h, cs(hf)],
                                                  QT[:, h, cs(hf)], tpk(hf))
                                return ins
                            P(mmkk, [bKT, bQT], [bpA, bpB])
                            V(lambda e: e.tensor_tensor(out=Dm[0:R1], in0=Dm[0:R1],
                                                        in1=cau.unsqueeze(1).to_broadcast([R1, 8, 64]), op=ALU.mult),
                              [bb["Dm"], bmasks], [bb["Dm"]])
                            yield
                            A(lambda e: e.activation(out=Dm[0:R1], in_=Dm[0:R1], func=AF.Exp), [bb["Dm"]], [bb["Dm"]])
                            for hf in hv:
                                G(lambda e, hf=hf: e.tensor_tensor(out=kg[:, hf], in0=KT[:, :, cs(hf)], in1=EG[:, hf],
                                                                   op=ALU.mult), [bKT, bEG], [bkg])
                            yield
                            G(lambda e: e.tensor_tensor(out=Ds[0:R1], in0=Dm[0:R1],
                                                        in1=stri.unsqueeze(1).to_broadcast([R1, 8, 64]), op=ALU.mult),
                              [bb["Dm"], bmasks], [bb["Ds"]])
                            G(lambda e: e.tensor_tensor(out=Dm[0:R1], in0=Dm[0:R1],
                                                        in1=cau.unsqueeze(1).to_broadcast([R1, 8, 64]), op=ALU.mult),
                              [bb["Dm"], bmasks], [bb["Dm"]])
                            yield
                            V(lambda e: e.tensor_tensor(out=Xm[0:R1], in0=v3(pA, 0, R1), in1=Ds[0:R1], op=ALU.mult),
                              [bpA, bb["Ds"]], [bb["Xm"]])
                            V(lambda e: e.tensor_tensor(out=Xm[0:R1], in0=Xm[0:R1],
                                                        in1=nbeta.unsqueeze(2).to_broadcast([R1, 8, 64]),
                                                        op=ALU.mult), [bb["Xm"], bsmA], [bb["Xm"]])
                            yield
                            A(lambda e: e.activation(out=Xb[0:R1], in_=Xm[0:R1], func=AF.Copy), [bb["Xm"]], [bb["Xb"]])
                            V(lambda e: e.tensor_tensor(out=qkm[0:R1], in0=v3(pB, 0, R1), in1=Dm[0:R1], op=ALU.mult),
                              [bpB, bb["Dm"]], [bqkm])
                            yield

                            def trx(e):
                                for hf in hv:
                                    for h in range(8):
                                        if hf:
                                            ins = e.transpose(out=PT[rs(hf), h, 0:64], in_=Xb[rs(hf), h, :],
                                                              identity=identb[64:128, 64:128], tile_position=(64, 64))
                                        else:
                                            ins = e.transpose(out=PT[rs(hf), h, 0:64], in_=Xb[rs(hf), h, :],
                                                              identity=identb[0:64, 0:64])
                                return ins
                            P(trx, [bb["Xb"], bconst], [bPT])
                            G(lambda e: e.tensor_tensor(
                                out=Tt[0:R1], in0=Xm[0:R1], in1=id2.unsqueeze(1).to_broadcast([R1, 8, 64]),
                                op=ALU.add), [bb["Xm"], bmasks], [bb["Tt"]])
                            yield
                            V(lambda e: e.tensor_copy(out=Wb[0:R1], in_=PT[0:R1, :, 0:64]), [bPT], [bb["Wb"]])
                            A(lambda e: e.activation(out=Ttb[0:R1], in_=Tt[0:R1], func=AF.Copy), [bb["Tt"]], [bTtb])
                            yield
                            for hf in hv:
                                G(lambda e, hf=hf: e.tensor_tensor(out=qg[:, hf], in0=QT[:, :, cs(hf)], in1=EG[:, hf],
                                                                   op=ALU.mult), [bQT, bEG], [bqg])
                            nlev = 2 if is_s else 5
                            alt = [(Pw1, Px1, bb["Pw1"], bb["Px1"]), (Wb, Xb, bb["Wb"], bb["Xb"])]

                            def do_mmsq(Pw, Px, bPw, bPx, lastl):
                                def mmsq(e):
                                    for hf in hv:
                                        for h in range(8):
                                            ins = mmx(e, pA[rs(hf), h * 64:(h + 1) * 64], Px[rs(hf), h, :],
                                                      Pw[rs(hf), h, :], tp(hf))
                                            if not lastl:
                                                ins = mmx(e, pB[rs(hf), h * 64:(h + 1) * 64], Pw[rs(hf), h, :],
                                                          Px[rs(hf), h, :], tp(hf))
                                    return ins
                                P(mmsq, [bPw, bPx], [bpA, bpB])

                            def do_copies(nPw, nPx, bnPw, bnPx, lastl):
                                A(lambda e: e.activation(out=nPw[0:R1], in_=v3(pA, 0, R1), func=AF.Copy),
                                  [bpA], [bnPw])
                                if not lastl:
                                    V(lambda e: e.tensor_copy(out=nPx[0:R1], in_=v3(pB, 0, R1)), [bpB], [bnPx])

                            def do_mminc(nPw, bnPw):
                                def mminc(e):
                                    for hf in hv:
                                        idb = identb[64:128, 64:128] if hf else identb[0:64, 0:64]
                                        for h in range(8):
                                            mmx(e, pC[rs(hf), h * 64:(h + 1) * 64], idb, Ttb[rs(hf), h, :], tp(hf),
                                                start=True, stop=False)
                                            ins = mmx(e, pC[rs(hf), h * 64:(h + 1) * 64], nPw[rs(hf), h, :],
                                                      Ttb[rs(hf), h, :], tp(hf), start=False, stop=True)
                                    return ins
                                P(mminc, [bnPw, bTtb, bconst], [bpC])

                            cur = (Wb, Xb, bb["Wb"], bb["Xb"])
                            do_mmsq(cur[0], cur[1], cur[2], cur[3], nlev == 1)
                            yield
                            nxt_ = alt[0]
                            do_copies(nxt_[0], nxt_[1], nxt_[2], nxt_[3], nlev == 1)
                            yield
                            for lev in range(1, nlev + 1):
                                lvl = alt[(lev - 1) % 2]
                                if lev < nlev:
                                    do_mmsq(lvl[0], lvl[1], lvl[2], lvl[3], lev + 1 == nlev)
                                do_mminc(lvl[0], lvl[2])
                                yield
                                if lev < nlev:
                                    nx2 = alt[lev % 2]
                                    do_copies(nx2[0], nx2[1], nx2[2], nx2[3], lev + 1 == nlev)
                                A(lambda e: e.activation(out=Ttb[0:R1], in_=v3(pC, 0, R1), func=AF.Copy), [bpC], [bTtb])
                                yield

                        def chain(ucol, is_s, hf, ut, p):
                            cs0 = ucol + 64 * hf
                            cs1 = cs0 + 64
                            r0, r1 = 64 * hf, 64 * hf + 64
                            EG, kg, qg = EGs[p][:, hf], kgs[p][:, hf], qgs[p][:, hf]
                            qkm, Ttb, kd, vtm = qkms[p], Ttbs[p], kds[p], vtms[p]
                            bEG, bkg, bqg, bqkm, bTtb, bkd, bvtm = (b2["EG"][p], b2["kg"][p],
                                                                    b2["qg"][p], b2["qkm"][p], b2["Ttb"][p],
                                                                    b2["kd"][p], b2["vtm"][p])
                            beta = smA[r0:r1, 1, ut, :]
                            cb = [(c1, bc1), (c2, bc2)]

                            def tv_and_vnew():
                                def mmtv(e):
                                    for h in range(8):
                                        ins = mmx(e, cb[h // 4][0][r0:r1, (h % 4) * 128:(h % 4 + 1) * 128],
                                                  Ttb[r0:r1, h, :], vd[r0:r1, h, :], tp(hf))
                                    return ins
                                P(mmtv, [bTtb, bb["vd"]], [bc1, bc2])
                                for hh in range(2):
                                    V(lambda e, hh=hh: e.tensor_tensor(
                                        out=vnew[r0:r1, hh * 4:(hh + 1) * 4, :], in0=v4(cb[hh][0], r0, r1),
                                        in1=beta[:, hh * 4:(hh + 1) * 4].unsqueeze(2).to_broadcast([64, 4, 128]),
                                        op=ALU.mult), [cb[hh][1], bsmA], [bb["vnew"]])

                            if not is_s:
                                def mmkgs(e):
                                    for h in range(8):
                                        ins = mmx(e, cb[h // 4][0][r0:r1, (h % 4) * 128:(h % 4 + 1) * 128],
                                                  kg[:, h, :], Sbf[:, h, :], tpk(hf))
                                    return ins
                                P(mmkgs, [bkg, bSbf], [bc1, bc2])
                                yield
                                for hh in range(2):
                                    V(lambda e, hh=hh: e.tensor_tensor(
                                        out=vd[r0:r1, hh * 4:(hh + 1) * 4, :], in0=vtm[r0:r1, hh * 4:(hh + 1) * 4, :],
                                        in1=v4(cb[hh][0], r0, r1), op=ALU.subtract), [cb[hh][1], bvtm], [bb["vd"]])
                                yield
                                tv_and_vnew()
                                yield

                                def mmo(e):
                                    for h in range(8):
                                        e.matmul(c3[:, h * 64:(h + 1) * 64], lhsT=Sbf[:, h, :], rhs=qg[:, h, :],
                                                 start=True, stop=False)
                                        ins = e.matmul(c3[:, h * 64:(h + 1) * 64], lhsT=vnew[r0:r1, h, :],
                                                       rhs=qkm[r0:r1, h, :], start=False, stop=True)
                                    return ins
                                P(mmo, [bSbf, bqg, bb["vnew"], bqkm], [bc3])

                                def mmsi(e):
                                    for h in range(8):
                                        ins = e.matmul(cb[h // 4][0][:, (h % 4) * 128:(h % 4 + 1) * 128],
                                                       lhsT=kd[r0:r1, h, :], rhs=vnew[r0:r1, h, :],
                                                       start=True, stop=True)
                                    return ins
                                P(mmsi, [bkd, bb["vnew"]], [bc1, bc2])
                                yield
                                for hh in range(2):
                                    G(lambda e, hh=hh: e.tensor_tensor(
                                        out=Sst[:, hh * 4:(hh + 1) * 4, :], in0=Sst[:, hh * 4:(hh + 1) * 4, :],
                                        in1=EG[:, hh * 4:(hh + 1) * 4, 63:64].to_broadcast([128, 4, 128]),
                                        op=ALU.mult), [bSst, bEG], [bSst])
                                    V(lambda e, hh=hh: e.tensor_tensor(
                                        out=Sst[:, hh * 4:(hh + 1) * 4, :], in0=Sst[:, hh * 4:(hh + 1) * 4, :],
                                        in1=v4(cb[hh][0], 0, 128), op=ALU.add), [bSst, cb[hh][1]], [bSst])
                                yield
                                A(lambda e: e.activation(out=Sbf[:], in_=Sst[:], func=AF.Copy), [bSst], [bSbf])
                                A(lambda e: e.activation(out=ocp[:], in_=c3[:, :], func=AF.Copy), [bc3], [bb["ocp"]])
                                osrc, bosrc = ocp, bb["ocp"]
                            else:
                                for s in range(16):
                                    Ssb, Sb16 = Ssbs[s % 2], Sb16s[s % 2]
                                    bSsb, bSb16 = b2["Ssb"][s % 2], b2["Sb16"][s % 2]
                                    S.dma("sp", lambda e, s=s, Ssb=Ssb: e.dma_start(
                                        out=Ssb[:], in_=sssm[s].rearrange("h k v -> k h v")), writes=[bSsb])
                                    A(lambda e, Ssb=Ssb, Sb16=Sb16: e.activation(out=Sb16[:], in_=Ssb[:], func=AF.Copy),
                                      [bSsb], [bSb16])

                                    def mms(e, s=s, Sb16=Sb16):
                                        for h in range(8):
                                            e.matmul(c1[:, h * 64 + 4 * s:h * 64 + 4 * s + 4], lhsT=Sb16[:, h, :],
                                                     rhs=kg[:, h, 4 * s:4 * s + 4], start=True, stop=True)
                                            ins = e.matmul(c2[:, h * 64 + 4 * s:h * 64 + 4 * s + 4],
                                                           lhsT=Sb16[:, h, :], rhs=qg[:, h, 4 * s:4 * s + 4],
                                                           start=True, stop=True)
                                        return ins
                                    P(mms, [bSb16, bkg, bqg], [bc1, bc2])
                                    yield
                                V(lambda e: e.tensor_tensor(out=vdT[:], in0=VT[:, :, cs0:cs1], in1=v3(c1, 0, 128),
                                                            op=ALU.subtract), [bVT, bc1], [bb["vdT"]])
                                A(lambda e: e.activation(out=oS[:], in_=c2[:, :], func=AF.Copy),
                                  [bc2], [bb["oS"]])
                                yield

                                def trvd(e):
                                    for h in range(8):
                                        ins = e.transpose(out=PT[0:64, h, :], in_=vdT[:, h, :], identity=identb[:])
                                    return ins
                                P(trvd, [bb["vdT"], bconst], [bPT])
                                yield
                                A(lambda e: e.activation(out=vd[0:64], in_=PT[0:64, :, :], func=AF.Copy),
                                  [bPT], [bb["vd"]])
                                yield
                                tv_and_vnew()
                                yield

                                def mmo(e):
                                    for h in range(8):
                                        ins = e.matmul(c3[:, h * 64:(h + 1) * 64], lhsT=vnew[0:64, h, :],
                                                       rhs=qkm[0:64, h, :], start=True, stop=True)
                                    return ins
                                P(mmo, [bb["vnew"], bqkm], [bc3])
                                yield
                                V(lambda e: e.tensor_tensor(out=oS[:], in0=oS[:], in1=c3[:, :], op=ALU.add),
                                  [bc3, bb["oS"]], [bb["oS"]])
                                for s in range(16):
                                    Ssb, kdm = Ssbs[s % 2], kdms[s % 2]
                                    bSsb, bkdm = b2["Ssb"][s % 2], b2["kdm"][s % 2]
                                    S.dma("sp", lambda e, s=s, Ssb=Ssb: e.dma_start(
                                        out=Ssb[:], in_=sssm[s].rearrange("h k v -> k h v")), writes=[bSsb])
                                    V(lambda e, s=s, kdm=kdm: e.tensor_scalar(
                                        out=kdm[:], in0=kd[0:64], scalar1=masks[0:64, MK_SEL + s:MK_SEL + s + 1],
                                        scalar2=None, op0=ALU.mult), [bkd, bmasks], [bkdm])

                                    def mmsi(e, kdm=kdm):
                                        for h in range(8):
                                            ins = e.matmul(cb[h // 4][0][:, (h % 4) * 128:(h % 4 + 1) * 128],
                                                           lhsT=kdm[:, h, :], rhs=vnew[0:64, h, :],
                                                           start=True, stop=True)
                                        return ins
                                    P(mmsi, [bkdm, bb["vnew"]], [bc1, bc2])
                                    for hh in range(2):
                                        V(lambda e, hh=hh, s=s, Ssb=Ssb: e.tensor_tensor(
                                            out=Ssb[:, hh * 4:(hh + 1) * 4, :], in0=Ssb[:, hh * 4:(hh + 1) * 4, :],
                                            in1=EG[:, hh * 4:(hh + 1) * 4, 4 * s + 3:4 * s + 4].to_broadcast(
                                                [128, 4, 128]), op=ALU.mult), [bSsb, bEG], [bSsb])
                                        V(lambda e, hh=hh, Ssb=Ssb: e.tensor_tensor(
                                            out=Ssb[:, hh * 4:(hh + 1) * 4, :], in0=Ssb[:, hh * 4:(hh + 1) * 4, :],
                                            in1=v4(cb[hh][0], 0, 128), op=ALU.add), [bSsb, cb[hh][1]], [bSsb])
                                    S.dma("sp", lambda e, s=s, Ssb=Ssb: e.dma_start(
                                        out=nss[s].rearrange("h k v -> k h v"), in_=Ssb[:]),
                                        reads=[bSsb], writes=[Buf()])
                                    yield
                                osrc, bosrc = oS, bb["oS"]
                            A(lambda e: e.activation(out=osq[:], in_=osrc[:, :], func=AF.Square), [bosrc], [bb["osq"]])
                            yield
                            P(lambda e: e.matmul(c3[:, :], lhsT=onesf[:], rhs=osq[:], start=True, stop=True),
                              [bb["osq"], bconst], [bc3])
                            yield
                            A(lambda e: e.activation(out=orr[:], in_=c3[:, :], func=AF.Ln, scale=1.0 / 128,
                                                     bias=RMS_EPS), [bc3], [bb["orr"]])
                            A(lambda e: e.activation(out=orr[:], in_=orr[:], func=AF.Exp, scale=-0.5),
                              [bb["orr"]], [bb["orr"]])
                            yield
                            V(lambda e: e.tensor_tensor(out=osq[:], in0=osrc[:, :], in1=orr[:], op=ALU.mult),
                              [bosrc, bb["orr"]], [bb["osq"]])
                            yield
                            V(lambda e: e.scalar_tensor_tensor(
                                out=ZT[:, :, cs0:cs1], in0=osq[:, :].rearrange("p (h i) -> p h i", h=8),
                                scalar=small[:, SM_NRM:SM_NRM + 1],
                                in1=ZT[:, :, cs0:cs1], op0=ALU.mult, op1=ALU.mult),
                              [bb["osq"], bsmall, bZT], [bZT])
                            yield

                        def run_all(g):
                            for _ in g:
                                pass

                        def interleave(gens):
                            gens = [g for g in gens if g is not None]
                            while gens:
                                for g in list(gens):
                                    try:
                                        next(g)
                                    except StopIteration:
                                        gens.remove(g)

                        def chain_unit(u, p):
                            ucol, uis, unh, ut = u
                            for hf in range(unh):
                                for _ in chain(ucol, uis, hf, ut, p):
                                    yield

                        run_all(pre(units[0][0], units[0][1], units[0][2], units[0][3], 0))
                        for ui, u in enumerate(units):
                            nxt = None
                            if ui + 1 < len(units):
                                un = units[ui + 1]
                                nxt = pre(un[0], un[1], un[2], un[3], (ui + 1) % 2)
                            if OVERLAP:
                                interleave([nxt, chain_unit(u, ui % 2)])
                            else:
                                run_all(chain_unit(u, ui % 2))
                                if nxt is not None:
                                    run_all(nxt)
                        if last:
                            S.dma("sp", lambda e: e.dma_start(out=nsp.rearrange("h k v -> k h v"), in_=Sst[:]),
                                  reads=[bSst], writes=[Buf()])
                        S.barrier()

                if stop_after == "bb2":
                    return
                with ExitStack() as sc:
                    GBT = sb("GBT", [128, 8, TG], F32, sc)
                    bGBT = Buf("GBT")
                    MTb = sb("MTb", [128, 8, TG], BF16, sc)
                    bMTb = Buf("MTb")
                    WsmC = sb("Wsm2", [128, 8, D], BF16, sc)
                    bWsmC = Buf("Wsm2")
                    t0C = sb("t0c", [128, 512], F32, sc)
                    bt0C = Buf("t0c")
                    S.dma("pool", lambda e: e.dma_start(out=WsmC[:], in_=w_o),
                          writes=[bWsmC], ring="wr", nring=8)
                    load_gb(2, 3)

                    def epi_gb(bi, c0, n, bank, bbank):
                        A(lambda e: e.activation(out=GBT[:, bi, c0:c0 + n], in_=bank[:, 0:n], func=AF.Sigmoid,
                                                 bias=small[:, SM_BFM + 48 + bi:SM_BFM + 49 + bi]),
                          [bbank, bsmall], [bGBT])
                    gemm_fm([w_in[48 + c] for c in range(8)], XT, xt_bufs,
                            pieces, epi_gb)

                    def epi_bb(bi, c0, n, bank, bbank):
                        V(lambda e: e.tensor_tensor(out=t0C[:, 0:n], in0=GBT[:, bi, c0:c0 + n], in1=bank[:, 0:n],
                                                    op=ALU.mult), [bbank, bGBT], [bt0C])
                        V(lambda e: e.tensor_tensor(out=MTb[:, bi, c0:c0 + n], in0=MT[:, bi, c0:c0 + n],
                                                    in1=t0C[:, 0:n], op=ALU.add), [bt0C, bMT], [bMTb])
                    gemm_fm([w_bb[c] for c in range(8)], ZT, lambda c0, n: [bZT],
                            pieces, epi_bb)
                    def wo_tile(li, col0, pt):
                        for half in range(2):
                            bank, bbank = next_bank()

                            def mmo2(e, bank=bank, col0=col0, pt=pt, half=half):
                                for k in range(8):
                                    ins = e.matmul(bank[0:pt, :], lhsT=MTb[:, k, col0:col0 + pt],
                                                   rhs=WsmC[:, k, half * 512:(half + 1) * 512],
                                                   start=(k == 0), stop=(k == 7))
                                return ins
                            P(mmo2, [bMTb, bWsmC], [bbank])
                            V(lambda e, bank=bank, li=li, pt=pt, half=half: e.scalar_tensor_tensor(
                                out=R[0:pt, li, half * 512:(half + 1) * 512],
                                in0=R[0:pt, li, half * 512:(half + 1) * 512], scalar=ALPHA, in1=bank[0:pt, :],
                                op0=ALU.mult, op1=ALU.add), [bbank, bR[li]], [bR[li]])

                    def fin2_tile(li, col0, pt):
                        layer_norm([(li, col0, pt)], lambda li_, pt_: R[0:pt_, li_, :],
                                   lambda li_, pt_: R[0:pt_, li_, :], bR, bR)
                        to_xt(li, col0, pt)
                    prev = None
                    for t_ in tl:
                        wo_tile(*t_)
                        if prev is not None:
                            fin2_tile(*prev)
                        prev = t_
                    fin2_tile(*prev)
                    S.barrier()

            ffn(1, final=True)

        tb_ = 0
        for gidx_, gsz_ in enumerate(groups):
            do_group(gidx_, gsz_, tb_)
            tb_ += gsz_

        S.barrier(final=True)
        S.emit()
        build.last_sched = S
    return nc


def _masks():
    m = np.zeros((128, NMK), np.float32)
    m[:, MK_ID:MK_ID + 128] = np.eye(128, dtype=np.float32)
    s = np.arange(128)[:, None]
    t = np.arange(128)[None, :]
    m[:, MK_GMP:MK_GMP + 128] = (s <= t)
    j = np.arange(64)[:, None]
    i = np.arange(64)[None, :]
    same = (j // 4) == (i // 4)
    m[0:64, MK_GMS:MK_GMS + 64] = (j <= i) & same
    m[0:64, MK_CAUP:MK_CAUP + 64] = (i >= j)
    m[0:64, MK_STRP:MK_STRP + 64] = (i > j)
    m[0:64, MK_CAUS:MK_CAUS + 64] = (i >= j) & same
    m[0:64, MK_STRS:MK_STRS + 64] = (i > j) & same
    m[0:64, MK_SUFP:MK_SUFP + 64] = (j > i)
    m[0:64, MK_SUFS:MK_SUFS + 64] = (j > i) & same
    m[0:64, MK_SEL:MK_SEL + 16] = (np.arange(64)[:, None] // 4) == np.arange(16)[None, :]
    m[0:64, MK_ID2:MK_ID2 + 64] = np.eye(64, dtype=np.float32)
    for c0 in (MK_CAUP, MK_STRP, MK_SUFP, MK_ID2):
        m[64:128, c0:c0 + 64] = m[0:64, c0:c0 + 64]
    return m


def _blocks(w, c0, nb):
    w = np.asarray(w, np.float32)[:, c0:c0 + nb * 128]
    return np.ascontiguousarray(w.reshape(8, 128, nb, 128).transpose(2, 1, 0, 3))


def _up_blocks(w):
    return np.concatenate([_blocks(w, 0, NJ), _blocks(w, DFF, NJ)])


def _kmajor(w):
    w = np.asarray(w, np.float32)
    return np.ascontiguousarray(w.reshape(8, 128, w.shape[1]).transpose(1, 0, 2))


def prep_shared(p):
    f = np.float32
    b_in = np.asarray(p["b_in"], f)
    small = np.zeros((128, NSM), f)

    def fm(v):
        return np.ascontiguousarray(v.reshape(-1, 128).T)
    small[:, 0:8] = fm(b_in[OFF_U:OFF_U + 1024])
    small[:, 8:32] = fm(b_in[OFF_Q:OFF_Q + 3072])
    small[:, 32:40] = fm(b_in[OFF_Z:OFF_Z + 1024])
    small[:, 40:48] = fm(b_in[OFF_GA:OFF_GA + 1024])
    small[:, 48:56] = fm(b_in[OFF_GB:OFF_GB + 1024])
    cw = np.asarray(p["dn_conv_w"], f)
    small[:, SM_CONV:SM_CONV + 96] = cw.reshape(4, 24, 128).transpose(2, 1, 0).reshape(128, 96)
    small[:, SM_NRM] = np.asarray(p["dn_norm_w"], f)
    small[:, SM_BBD:SM_BBD + 16] = b_in[OFF_BD:OFF_BD + 16][None, :]
    small[:, SM_ALOG:SM_ALOG + 8] = np.asarray(p["dn_a_log"], f)[None, :]
    small[:, SM_DTB:SM_DTB + 8] = np.asarray(p["dn_dt_bias"], f)[None, :]
    vecs = [p["ln1_g"], p["ln1_b"], p["ln2_g"], p["ln2_b"], p["ln3_g"], p["ln3_b"], p["gm_v_g"], p["gm_v_b"],
            b_in[OFF_V:OFF_V + 1024]]
    lnbc = np.ascontiguousarray(np.broadcast_to(np.stack([np.asarray(v, f) for v in vecs])[:, None, :],
                                                (9, 128, D)))
    ws = np.asarray(p["gm_w_s"], f)
    wsT = np.ascontiguousarray(ws.transpose(2, 0, 1))
    idx = np.arange(64) % 4
    wsTs = np.ascontiguousarray(ws[:, idx][:, :, idx].transpose(2, 0, 1))
    bs = np.asarray(p["gm_b_s"], f)
    bsr = np.ascontiguousarray(bs[None, :, :])
    bsrs = np.ascontiguousarray(bs[:, idx][None, :, :])
    return {
        "w_up1": _up_blocks(p["ffn1_w_up"]), "w_dn1": np.ascontiguousarray(p["ffn1_w_down"], f),
        "w_up2": _up_blocks(p["ffn2_w_up"]), "w_dn2": np.ascontiguousarray(p["ffn2_w_down"], f),
        "w_in": np.concatenate([_blocks(p["w_in"], OFF_U, 8), _blocks(p["w_in"], OFF_Q, 24),
                                _blocks(p["w_in"], OFF_Z, 8), _blocks(p["w_in"], OFF_GA, 8),
                                _blocks(p["w_in"], OFF_GB, 8)]),
        "w_v": _kmajor(np.asarray(p["w_in"], f)[:, OFF_V:OFF_V + D]),
        "w_bd": _kmajor(np.asarray(p["w_in"], f)[:, OFF_BD:OFF_BD + 16]),
        "w_ba": _blocks(p["w_branch_a"], 0, 8), "w_bb": _blocks(p["w_branch_b"], 0, 8),
        "w_o": _kmajor(np.asarray(p["w_out"], f)),
        "lnbc": lnbc, "small": small, "masks": _masks(), "wsT": wsT, "wsTs": wsTs, "bsr": bsr, "bsrs": bsrs,
    }


PARAM_KEYS = ["ffn1_w_up", "ffn1_w_down", "ln1_g", "ln1_b", "w_in", "b_in", "gm_v_g", "gm_v_b", "gm_w_s", "gm_b_s",
              "dn_conv_w", "dn_a_log", "dn_dt_bias", "dn_norm_w", "w_branch_a", "w_branch_b", "w_out", "ln2_g",
              "ln2_b", "ffn2_w_up", "ffn2_w_down", "ln3_g", "ln3_b"]

GROUPS = [4, 4, 4, 4]
_NC_CACHE = {}


def kernel(**inputs):
    f = np.float32
    x_prompt = np.asarray(inputs["x_prompt"], f)
    x_sample = np.asarray(inputs["x_sample"], f)
    state_conv = np.asarray(inputs["state_conv"], f)[0]
    state_ssm = np.asarray(inputs["state_ssm"], f)[0]
    p = {k: np.asarray(inputs[k], f)[0] for k in PARAM_KEYS}
    shared = prep_shared(p)
    n = 8
    B, T, _ = x_prompt.shape
    assert B == n and T == 2048
    key = "full"
    if key not in _NC_CACHE:
        _NC_CACHE[key] = build(T // 128, GROUPS, has_s=True)
    nc = _NC_CACHE[key]
    in_maps = []
    for c in range(n):
        m = dict(shared)
        m["xp"] = np.ascontiguousarray(x_prompt[c])
        m["xs"] = np.ascontiguousarray(x_sample[c * 16:(c + 1) * 16].reshape(64, D))
        m["sconv"] = np.ascontiguousarray(state_conv[c * 16:(c + 1) * 16].reshape(48, 3072))
        m["sssm"] = np.ascontiguousarray(state_ssm[c * 16:(c + 1) * 16])
        in_maps.append(m)
    res = run_bass_kernel_spmd(nc, in_maps, core_ids=list(range(n)))
    r = res.results
    y_p = np.stack([r[c]["yp"] for c in range(n)]).astype(f)
    y_s = np.concatenate([r[c]["ys"].reshape(16, 4, D) for c in range(n)]).astype(f)
    c_p = np.stack([r[c]["ncp"] for c in range(n)])[None].astype(f)
    s_p = np.stack([r[c]["nsp"] for c in range(n)])[None].astype(f)
    c_s = np.concatenate([r[c]["ncs"].reshape(16, 3, 3072) for c in range(n)])[None].astype(f)
    s_s = np.concatenate([r[c]["nss"] for c in range(n)])[None].astype(f)
    v_s = np.concatenate([r[c]["ngv"].reshape(16, 4, D) for c in range(n)])[None].astype(f)
    return (y_p, y_s, c_p, s_p, c_s, s_s, v_s)
```
